# Optimizing a Trainium2 kernel written in Bass

```python
import math
import jax, jax.numpy as jnp
from jax import lax
import numpy as np

D_MODEL = 4096
BATCH = 16
SEQ = 256
DEPTH = 2
DEC_BATCH = 8
DEC_SEQ = 2048
PAST_LEN = 256

GRID_W = 64
HEAD_DIM = 128
Q_BLOCK = 128
WINDOW = 128
ROPE_BASE = 10000.0
EPS = 1e-6
NEG_INF = -1e30

A_HEADS = 24
A_KV_HEADS = 6
A_GROUP = A_HEADS // A_KV_HEADS
POOL_WINDOWS = (2, 4, 8, 16)
POOL_GROUPS = len(POOL_WINDOWS)
POOL_GROUP_DIM = 256
POOL_DIM = POOL_GROUPS * POOL_GROUP_DIM
EVEN_IN = (A_HEADS + 2 * A_KV_HEADS) * HEAD_DIM + POOL_DIM
EVEN_MIX = A_HEADS * HEAD_DIM + POOL_DIM
C_HEADS = 16
C_Q_RANK = 1024
C_KV_RANK = 512
C_NOPE = 128
C_ROPE = 64
C_V = 128
D_HEADS = 8
D_DK = 128
D_DV = 2 * D_DK
ODD_IN = C_Q_RANK + C_KV_RANK + C_ROPE + 2 * D_HEADS * 2 * D_DK + D_HEADS * D_DV
ODD_MIX = C_HEADS * C_V + D_HEADS * D_DV
FF_DIM = 11008
N_EVEN = (DEPTH + 1) // 2
N_ODD = DEPTH // 2

kernel_name = 'hybrid_prefix_diffusion_step'


def rms_norm(x, g):
    xf = x.astype(jnp.float32)
    y = xf * lax.rsqrt(jnp.mean(xf * xf, axis=-1, keepdims=True) + EPS)
    return (y * g.astype(jnp.float32)).astype(x.dtype)


def _rope_1d(x, pos):
    half = x.shape[-1] // 2
    inv = ROPE_BASE ** (-jnp.arange(half, dtype=jnp.float32) / half)
    ang = pos.astype(jnp.float32)[:, None] * inv[None, :]
    cos = jnp.cos(ang)[None, :, None, :]
    sin = jnp.sin(ang)[None, :, None, :]
    xf = x.astype(jnp.float32)
    x1, x2 = xf[..., :half], xf[..., half:]
    return jnp.concatenate([x1 * cos - x2 * sin, x2 * cos + x1 * sin], axis=-1).astype(x.dtype)


def rope_2d(x):
    S = x.shape[1]
    rows = S // GRID_W
    row = jnp.repeat(jnp.arange(rows), GRID_W)
    col = jnp.tile(jnp.arange(GRID_W), rows)
    d = x.shape[-1] // 2
    return jnp.concatenate([_rope_1d(x[..., :d], row), _rope_1d(x[..., d:], col)], axis=-1)


def to_blocks(x):
    B, S = x.shape[:2]
    return jnp.moveaxis(x.reshape(B, S // Q_BLOCK, Q_BLOCK, *x.shape[2:]), 1, 0)


def from_blocks(y):
    n, B, qb = y.shape[:3]
    return jnp.moveaxis(y, 0, 1).reshape(B, n * qb, *y.shape[3:])


def band_windows(x):
    B, S = x.shape[:2]
    n = S // Q_BLOCK
    xp = jnp.pad(x, [(0, 0), (Q_BLOCK, Q_BLOCK)] + [(0, 0)] * (x.ndim - 2))
    xb = jnp.moveaxis(xp.reshape(B, n + 2, Q_BLOCK, *x.shape[2:]), 1, 0)
    return jnp.concatenate([xb[:n], xb[1:n + 1], xb[2:]], axis=2)


def softmax_parts(logits, sink=None):
    sizes = [l.shape[-1] for l in logits]
    parts = list(logits)
    if sink is not None:
        parts.append(jnp.broadcast_to(sink, logits[0].shape[:-1] + (1,)))
    p = jax.nn.softmax(jnp.concatenate(parts, axis=-1), axis=-1)
    out, off = [], 0
    for s in sizes:
        out.append(p[..., off:off + s])
        off += s
    return out


def window_gqa_context(q, k, v, sink):
    scale = HEAD_DIM ** -0.5
    sink_l = sink.astype(jnp.float32).reshape(1, A_KV_HEADS, A_GROUP, 1, 1)

    def block(qb):
        s = jnp.einsum('bqkgd,bckd->bkgqc', qb, k, preferred_element_type=jnp.float32) * scale
        (p,) = softmax_parts([s], sink_l)
        return jnp.einsum('bkgqc,bckd->bqkgd', p.astype(v.dtype), v)

    return from_blocks(lax.map(block, to_blocks(q)))


def window_gqa_latent(q, k, v, sink, ctx_k, ctx_v):
    S = q.shape[1]
    scale = HEAD_DIM ** -0.5
    sink_l = sink.astype(jnp.float32).reshape(1, A_KV_HEADS, A_GROUP, 1, 1)
    kw, vw = band_windows(k), band_windows(v)
    kvalid = band_windows(jnp.ones((1, S), dtype=bool))[:, 0]
    rel = jnp.arange(3 * Q_BLOCK)[None, :] - Q_BLOCK - jnp.arange(Q_BLOCK)[:, None]
    band = jnp.abs(rel) <= WINDOW

    def block(args):
        qb, kb, vb, valid = args
        s_lat = jnp.einsum('bqkgd,bjkd->bkgqj', qb, kb, preferred_element_type=jnp.float32) * scale
        s_lat = jnp.where(band & valid[None, :], s_lat, NEG_INF)
        s_ctx = jnp.einsum('bqkgd,bckd->bkgqc', qb, ctx_k, preferred_element_type=jnp.float32) * scale
        p_lat, p_ctx = softmax_parts([s_lat, s_ctx], sink_l)
        return (jnp.einsum('bkgqj,bjkd->bqkgd', p_lat.astype(vb.dtype), vb)
                + jnp.einsum('bkgqc,bckd->bqkgd', p_ctx.astype(ctx_v.dtype), ctx_v))

    return from_blocks(lax.map(block, (to_blocks(q), kw, vw, kvalid)))


def pool_mixer(u, w_pool, pool_scale):
    B, S, _ = u.shape
    ug = u.reshape(B, S, POOL_GROUPS, POOL_GROUP_DIM)
    cs = jnp.pad(jnp.cumsum(ug.astype(jnp.float32), axis=1), ((0, 0), (1, 0), (0, 0), (0, 0)))
    half = jnp.array(POOL_WINDOWS, dtype=jnp.int32) // 2
    t = jnp.arange(S, dtype=jnp.int32)
    lo = jnp.clip(t[:, None] - half[None, :], 0, S)
    hi = jnp.clip(t[:, None] + half[None, :], 0, S)
    gidx = jnp.arange(POOL_GROUPS)[None, :]
    total = cs[:, hi, gidx] - cs[:, lo, gidx]
    count = (hi - lo).astype(jnp.float32)[None, :, :, None]
    pooled = (total / count - ug.astype(jnp.float32)).astype(u.dtype)
    y = jnp.einsum('bsgc,gcd->bsgd', pooled, w_pool)
    return y.reshape(B, S, POOL_DIM) * pool_scale


def even_mixer(h, w_in, w_out, sink, w_pool, pool_scale, ctx):
    B, S, _ = h.shape
    qa, kva = A_HEADS * HEAD_DIM, A_KV_HEADS * HEAD_DIM
    q, k, v, u = jnp.split(h @ w_in, [qa, qa + kva, qa + 2 * kva], axis=-1)
    q = q.reshape(B, S, A_HEADS, HEAD_DIM)
    k = k.reshape(B, S, A_KV_HEADS, HEAD_DIM)
    v = v.reshape(B, S, A_KV_HEADS, HEAD_DIM)
    if ctx is None:
        a = window_gqa_context(q.reshape(B, S, A_KV_HEADS, A_GROUP, HEAD_DIM), k, v, sink)
    else:
        q = rope_2d(q)
        k = rope_2d(k)
        a = window_gqa_latent(q.reshape(B, S, A_KV_HEADS, A_GROUP, HEAD_DIM), k, v, sink, ctx[0], ctx[1])
    b = pool_mixer(u, w_pool, pool_scale)
    y = jnp.concatenate([a.reshape(B, S, qa), b], axis=-1) @ w_out
    return y, (k, v)


def mla_attend(cq, ckv, kr, g_qn, w_uq, w_uk, w_uv, ctx_ckv, ctx_kr):
    B, S, _ = cq.shape
    q = (rms_norm(cq, g_qn) @ w_uq).reshape(B, S, C_HEADS, C_NOPE + C_ROPE)
    q_nope, q_rope = q[..., :C_NOPE], q[..., C_NOPE:]
    if ctx_ckv is not None:
        q_rope = rope_2d(q_rope)
        kr = rope_2d(kr[:, :, None, :])[:, :, 0]
    keysets = [(jnp.einsum('bsl,lhd->bshd', ckv, w_uk), kr, jnp.einsum('bsl,lhd->bshd', ckv, w_uv))]
    if ctx_ckv is not None:
        keysets.append((jnp.einsum('bsl,lhd->bshd', ctx_ckv, w_uk), ctx_kr,
                        jnp.einsum('bsl,lhd->bshd', ctx_ckv, w_uv)))
    scale = (C_NOPE + C_ROPE) ** -0.5

    def block(args):
        qn, qr = args
        logits = [(jnp.einsum('bqhd,bkhd->bhqk', qn, kn, preferred_element_type=jnp.float32)
                   + jnp.einsum('bqhr,bkr->bhqk', qr, kre, preferred_element_type=jnp.float32)) * scale
                  for kn, kre, _ in keysets]
        ps = softmax_parts(logits)
        out = jnp.einsum('bhqk,bkhd->bqhd', ps[0].astype(keysets[0][2].dtype), keysets[0][2])
        for p, (_, _, vv) in zip(ps[1:], keysets[1:]):
            out = out + jnp.einsum('bhqk,bkhd->bqhd', p.astype(vv.dtype), vv)
        return out

    return from_blocks(lax.map(block, (to_blocks(q_nope), to_blocks(q_rope))))


def diff_attend(q, k, v, lam, ctx_k, ctx_v):
    B, S = q.shape[:2]
    if ctx_k is not None:
        q = rope_2d(q.reshape(B, S, D_HEADS * 2, D_DK)).reshape(B, S, D_HEADS, 2, D_DK)
        k = rope_2d(k.reshape(B, S, D_HEADS * 2, D_DK)).reshape(B, S, D_HEADS, 2, D_DK)
    keysets = [(k, v)]
    if ctx_k is not None:
        keysets.append((ctx_k, ctx_v))
    scale = D_DK ** -0.5

    def block(qb):
        logits = [jnp.einsum('bqhmd,bkhmd->bhmqk', qb, kk, preferred_element_type=jnp.float32) * scale
                  for kk, _ in keysets]
        ps = softmax_parts(logits)
        out = None
        for p, (_, vv) in zip(ps, keysets):
            w = (p[:, :, 0] - lam * p[:, :, 1]).astype(vv.dtype)
            o = jnp.einsum('bhqk,bkhe->bqhe', w, vv)
            out = o if out is None else out + o
        return out

    return from_blocks(lax.map(block, to_blocks(q)))


def odd_mixer(h, w_in, w_out, g_qn, w_uq, g_kvn, w_uk, w_uv, lq1, lk1, lq2, lk2, g_subln, lam_init, ctx):
    B, S, _ = h.shape
    dqk = D_HEADS * 2 * D_DK
    c0 = C_Q_RANK
    c1 = c0 + C_KV_RANK
    c2 = c1 + C_ROPE
    cq, ckv, kr, dq, dk, dv = jnp.split(h @ w_in, [c0, c1, c2, c2 + dqk, c2 + 2 * dqk], axis=-1)
    ckv = rms_norm(ckv, g_kvn)
    dq = dq.reshape(B, S, D_HEADS, 2, D_DK)
    dk = dk.reshape(B, S, D_HEADS, 2, D_DK)
    dv = dv.reshape(B, S, D_HEADS, D_DV)
    lam = (jnp.exp(jnp.sum(lq1.astype(jnp.float32) * lk1.astype(jnp.float32)))
           - jnp.exp(jnp.sum(lq2.astype(jnp.float32) * lk2.astype(jnp.float32))) + lam_init)
    if ctx is None:
        c_out = mla_attend(cq, ckv, kr, g_qn, w_uq, w_uk, w_uv, None, None)
        d_out = diff_attend(dq, dk, dv, lam, None, None)
    else:
        c_out = mla_attend(cq, ckv, kr, g_qn, w_uq, w_uk, w_uv, ctx[0], ctx[1])
        d_out = diff_attend(dq, dk, dv, lam, ctx[2], ctx[3])
    d_out = rms_norm(d_out, g_subln) * (1.0 - lam_init)
    y = jnp.concatenate([c_out.reshape(B, S, C_HEADS * C_V), d_out.reshape(B, S, D_HEADS * D_DV)], axis=-1) @ w_out
    return y, (ckv, kr, dk, dv)


def conv_ffn(h, w_up, conv_w, conv_b, w_down):
    z = h @ w_up
    zp = jnp.pad(z, ((0, 0), (1, 1), (0, 0)))
    z = zp[:, :-2] * conv_w[0] + zp[:, 1:-1] * conv_w[1] + zp[:, 2:] * conv_w[2] + conv_b
    gate, val = jnp.split(z, 2, axis=-1)
    return (jax.nn.silu(gate) * val) @ w_down


def setup_inputs(seed: int = 0) -> dict:
    key = jax.random.key(seed)
    ks = iter(jax.random.split(key, 40))
    D = D_MODEL

    def nrm(shape, scale):
        return jax.random.normal(next(ks), shape, jnp.float32) * scale

    def gain(shape):
        return 1.0 + nrm(shape, 0.02)

    return {
        'x_prompt': nrm((BATCH, SEQ, D), 1.0),
        'x_sample': nrm((DEC_BATCH, DEC_SEQ, D), 1.0),
        'cache_a_k': nrm((DEC_BATCH, N_EVEN, PAST_LEN, A_KV_HEADS, HEAD_DIM), 1.0),
        'cache_a_v': nrm((DEC_BATCH, N_EVEN, PAST_LEN, A_KV_HEADS, HEAD_DIM), 1.0),
        'cache_c_ckv': nrm((DEC_BATCH, N_ODD, PAST_LEN, C_KV_RANK), 1.0),
        'cache_c_krope': nrm((DEC_BATCH, N_ODD, PAST_LEN, C_ROPE), 1.0),
        'cache_d_k': nrm((DEC_BATCH, N_ODD, PAST_LEN, D_HEADS, 2, D_DK), 1.0),
        'cache_d_v': nrm((DEC_BATCH, N_ODD, PAST_LEN, D_HEADS, D_DV), 1.0),
        'c': nrm((DEC_BATCH, D), 1.0),
        'c_ctx': nrm((D,), 1.0),
        'w_mod': nrm((DEPTH, D, 6 * D), 0.5 * D ** -0.5),
        'b_mod': nrm((DEPTH, 6 * D), 0.01),
        'g_pre_mix': gain((DEPTH, D)),
        'g_post_mix': gain((DEPTH, D)),
        'g_pre_ffn': gain((DEPTH, D)),
        'g_post_ffn': gain((DEPTH, D)),
        'w_in_even': nrm((N_EVEN, D, EVEN_IN), D ** -0.5),
        'w_out_even': nrm((N_EVEN, EVEN_MIX, D), EVEN_MIX ** -0.5),
        'a_sink': nrm((N_EVEN, A_HEADS), 0.5),
        'w_pool': nrm((N_EVEN, POOL_GROUPS, POOL_GROUP_DIM, POOL_GROUP_DIM), POOL_GROUP_DIM ** -0.5),
        'pool_scale': gain((N_EVEN, POOL_DIM)),
        'w_in_odd': nrm((N_ODD, D, ODD_IN), D ** -0.5),
        'w_out_odd': nrm((N_ODD, ODD_MIX, D), ODD_MIX ** -0.5),
        'g_q_norm': gain((N_ODD, C_Q_RANK)),
        'w_uq': nrm((N_ODD, C_Q_RANK, C_HEADS * (C_NOPE + C_ROPE)), C_Q_RANK ** -0.5),
        'g_kv_norm': gain((N_ODD, C_KV_RANK)),
        'w_uk': nrm((N_ODD, C_KV_RANK, C_HEADS, C_NOPE), C_KV_RANK ** -0.5),
        'w_uv': nrm((N_ODD, C_KV_RANK, C_HEADS, C_V), C_KV_RANK ** -0.5),
        'lambda_q1': nrm((N_ODD, D_DK), 0.1),
        'lambda_k1': nrm((N_ODD, D_DK), 0.1),
        'lambda_q2': nrm((N_ODD, D_DK), 0.1),
        'lambda_k2': nrm((N_ODD, D_DK), 0.1),
        'g_subln': gain((N_ODD, D_DV)),
        'w_up': nrm((DEPTH, D, 2 * FF_DIM), D ** -0.5),
        'conv_w': nrm((DEPTH, 3, 2 * FF_DIM), 3 ** -0.5),
        'conv_b': nrm((DEPTH, 2 * FF_DIM), 0.01),
        'w_down': nrm((DEPTH, FF_DIM, D), FF_DIM ** -0.5),
    }


def reference(x_prompt, x_sample, cache_a_k, cache_a_v, cache_c_ckv, cache_c_krope, cache_d_k, cache_d_v,
              c, c_ctx, w_mod, b_mod, g_pre_mix, g_post_mix, g_pre_ffn, g_post_ffn,
              w_in_even, w_out_even, a_sink, w_pool, pool_scale,
              w_in_odd, w_out_odd, g_q_norm, w_uq, g_kv_norm, w_uk, w_uv,
              lambda_q1, lambda_k1, lambda_q2, lambda_k2, g_subln,
              w_up, conv_w, conv_b, w_down):

    def layer(i, x, cond, ctx):
        mods = jax.nn.silu(cond) @ w_mod[i] + b_mod[i]
        sh_m, sc_m, g_m, sh_f, sc_f, g_f = [m[:, None, :] for m in jnp.split(mods, 6, axis=-1)]
        h = rms_norm(x, g_pre_mix[i]) * (1.0 + sc_m) + sh_m
        j = i // 2
        if i % 2 == 0:
            y, st = even_mixer(h, w_in_even[j], w_out_even[j], a_sink[j], w_pool[j], pool_scale[j], ctx)
        else:
            lam_init = 0.8 - 0.6 * math.exp(-0.3 * i)
            y, st = odd_mixer(h, w_in_odd[j], w_out_odd[j], g_q_norm[j], w_uq[j], g_kv_norm[j], w_uk[j], w_uv[j],
                              lambda_q1[j], lambda_k1[j], lambda_q2[j], lambda_k2[j], g_subln[j], lam_init, ctx)
        x = x + g_m * rms_norm(y, g_post_mix[i])
        h = rms_norm(x, g_pre_ffn[i]) * (1.0 + sc_f) + sh_f
        x = x + g_f * rms_norm(conv_ffn(h, w_up[i], conv_w[i], conv_b[i], w_down[i]), g_post_ffn[i])
        return x, st

    xp = x_prompt
    cond_ctx = c_ctx[None, :]
    ak, av, cc, ckr, dkl, dvl = [], [], [], [], [], []
    for i in range(DEPTH):
        xp, st = layer(i, xp, cond_ctx, None)
        if i % 2 == 0:
            ak.append(st[0])
            av.append(st[1])
        else:
            cc.append(st[0])
            ckr.append(st[1])
            dkl.append(st[2])
            dvl.append(st[3])

    xs = x_sample
    for i in range(DEPTH):
        j = i // 2
        if i % 2 == 0:
            ctx = (cache_a_k[:, j], cache_a_v[:, j])
        else:
            ctx = (cache_c_ckv[:, j], cache_c_krope[:, j], cache_d_k[:, j], cache_d_v[:, j])
        xs, _ = layer(i, xs, c, ctx)

    new_a_k = jnp.stack(ak, axis=1)
    new_a_v = jnp.stack(av, axis=1)
    new_c_ckv = jnp.stack(cc, axis=1)
    new_c_krope = jnp.stack(ckr, axis=1)
    new_d_k = jnp.stack(dkl, axis=1)
    new_d_v = jnp.stack(dvl, axis=1)
    return (xp, xs, new_a_k, new_a_v, new_c_ckv, new_c_krope, new_d_k, new_d_v)
```

```python
import math
import numpy as np
import concourse.bass as bass
import concourse.mybir as mybir
from concourse.bass_utils import run_bass_kernel_spmd
from contextlib import ExitStack

F32 = mybir.dt.float32
BF16 = mybir.dt.bfloat16
ALU = mybir.AluOpType
AF = mybir.ActivationFunctionType
AX = mybir.AxisListType

NCORES = 8
D = 4096
KC = 32
T = 2560
EPS = 1e-6
FF = 11008
EVEN_IN = 5632
ODD_IN = 7744
ROPE_BASE = 10000.0
SHARD_WEIGHTS = False

ENGS = ("pe", "act", "dve", "pool", "sp")
NRING = 4
NDMA = 24


class Buf:
    __slots__ = ("name", "lw", "rd", "ro")

    def __init__(self, name="", ro=False):
        self.name = name
        self.lw = None
        self.rd = []
        self.ro = ro


class Op:
    __slots__ = ("eng", "fn", "deps", "sig", "n", "is_dma", "dsem", "dval", "q", "inc")

    def __init__(self, eng, fn, is_dma=False):
        self.q = eng
        self.inc = 16
        self.eng = eng
        self.fn = fn
        self.deps = []
        self.sig = False
        self.n = -1
        self.is_dma = is_dma
        self.dsem = -1
        self.dval = 0


class Prog:
    def __init__(self, nc):
        self.nc = nc
        self.ops = {e: [] for e in ENGS}
        self.ndma = {e: 0 for e in ENGS}
        self.bar = {e: None for e in ENGS}
        self.last_dmas = {e: [] for e in ENGS}
        self.qeng = {e: e for e in ENGS}

    def op(self, eng, fn, reads=(), writes=(), is_dma=False, q=None, inc=16):
        o = Op(eng, fn, is_dma)
        if q is not None:
            o.q = q
            o.inc = inc
            if q not in self.ndma:
                self.ndma[q] = 0
                self.last_dmas[q] = []
                self.qeng[q] = eng
        deps = []
        if self.bar[eng] is not None:
            deps.extend(self.bar[eng])
            self.bar[eng] = None
        for r in reads:
            if r.lw is not None:
                deps.append(r.lw)
        for w in writes:
            if w.lw is not None:
                deps.append(w.lw)
            deps.extend(w.rd)
        seen = set()
        for d in deps:
            if id(d) in seen or d is o:
                continue
            seen.add(id(d))
            if d.eng == eng and not d.is_dma and eng == "pe":
                continue
            if not d.is_dma:
                d.sig = True
            o.deps.append(d)
        for w in writes:
            w.lw = o
            w.rd = []
        for r in reads:
            if r.ro:
                continue
            if not is_dma:
                r.rd = [x for x in r.rd if x.is_dma or x.eng != eng]
            r.rd.append(o)
        if is_dma:
            k = self.ndma[o.q]
            self.ndma[o.q] += 1
            o.dsem = k % NDMA
            o.dval = o.inc * (k // NDMA + 1)
            ld = self.last_dmas[o.q]
            ld.append(o)
            if len(ld) > NDMA:
                ld.pop(0)
        self.ops[eng].append(o)
        return o

    def dma(self, eng, out, in_, reads=(), writes=()):
        return self.op(eng, lambda e: e.dma_start(out=out, in_=in_, allow_slow_non_contiguous=True), reads, writes, is_dma=True)

    def barrier(self):
        deps = []
        for e in ENGS:
            last = None
            for o in reversed(self.ops[e]):
                if not o.is_dma:
                    last = o
                    break
            if last is not None:
                last.sig = True
                deps.append(last)
        for q in self.last_dmas:
            if q != "cc":
                deps.extend(self.last_dmas[q])
        for e in ENGS:
            self.bar[e] = list(deps)

    def emit(self):
        nc = self.nc
        with ExitStack() as es:
            esem = {e: [es.enter_context(nc.semaphore(f"s_{e}{i}")) for i in range(NRING)] for e in ENGS}
            dsem = {q: [es.enter_context(nc.semaphore(f"d_{q}{i}")) for i in range(NDMA)]
                    for q in self.ndma if self.ndma[q] > 0}
            for e in ENGS:
                c = 0
                for o in self.ops[e]:
                    if o.sig and not o.is_dma:
                        o.n = c
                        c += 1
            block = es.enter_context(nc.Block())
            handles = {"pe": block.tensor, "act": block.scalar, "dve": block.vector,
                       "pool": block.gpsimd, "sp": block.sync}

            def make(e):
                ops = self.ops[e]

                def body(eng):
                    seen_e = {x: -1 for x in ENGS}
                    seen_d = {}
                    for o in ops:
                        for d in o.deps:
                            if d.is_dma:
                                key = (d.q, d.dsem)
                                if seen_d.get(key, 0) < d.dval:
                                    eng.wait_ge(dsem[d.q][d.dsem], d.dval)
                                    seen_d[key] = d.dval
                            else:
                                if seen_e[d.eng] < d.n:
                                    eng.wait_ge(esem[d.eng][d.n % NRING], d.n // NRING + 1)
                                    seen_e[d.eng] = d.n
                        if o.is_dma:
                            key = (o.q, o.dsem)
                            if o.dval > o.inc and seen_d.get(key, 0) < o.dval - o.inc:
                                eng.wait_ge(dsem[o.q][o.dsem], o.dval - o.inc)
                                seen_d[key] = o.dval - o.inc
                            ins = o.fn(eng)
                            ins.then_inc(dsem[o.q][o.dsem], o.inc)
                        else:
                            ins = o.fn(eng)
                            if o.sig:
                                ins.then_inc(esem[e][o.n % NRING], 1)
                    for q in dsem:
                        if self.qeng[q] != e:
                            continue
                        k = self.ndma[q]
                        qinc = 16 if q in ENGS else 1
                        for i in range(min(k, NDMA)):
                            val = qinc * ((k - 1 - i) // NDMA + 1)
                            if seen_d.get((q, i), 0) < val:
                                eng.wait_ge(dsem[q][i], val)
                return body

            for e in ENGS:
                if self.ops[e]:
                    handles[e](make(e))


class Arena:
    def __init__(self, ap, words):
        self.ap = ap
        self.words = words
        self.off = 0
        self.base = 0

    def alloc(self, shape, dt=F32):
        assert shape[0] <= 128
        n = int(np.prod(shape[1:]))
        nw = n if dt == F32 else (n + 1) // 2
        nw = (nw + 7) // 8 * 8
        assert self.off + nw <= self.words, f"SBUF arena overflow {self.off}+{nw}>{self.words}"
        a = self.ap[0:shape[0], self.off:self.off + nw]
        self.off += nw
        if dt != F32:
            a = a.bitcast(dt)
        a = a[:, 0:n]
        if len(shape) == 3:
            a = a.rearrange("p (a b) -> p a b", a=shape[1])
        elif len(shape) == 4:
            a = a.rearrange("p (a b c) -> p a b c", a=shape[1], b=shape[2])
        return a

    def mark(self):
        self.base = self.off

    def reset(self):
        self.off = self.base


def sgs(s):
    return s + 256 * (s // 1024)


def rope_tables(dim, S=2048, grid_w=64):
    d2 = dim // 2
    half = d2 // 2
    inv = ROPE_BASE ** (-np.arange(half, dtype=np.float32) / half)
    t = np.arange(S)
    row = (t // grid_w).astype(np.float32)
    col = (t % grid_w).astype(np.float32)
    cos = np.zeros((dim, S), np.float32)
    sin = np.zeros((dim, S), np.float32)
    R = np.zeros((dim, dim), np.float32)
    for part, pos in ((0, row), (1, col)):
        base = part * d2
        ang = pos[None, :] * inv[:, None]
        c = np.cos(ang).astype(np.float32)
        s = np.sin(ang).astype(np.float32)
        cos[base:base + half] = c
        cos[base + half:base + d2] = c
        sin[base:base + half] = s
        sin[base + half:base + d2] = s
        for j in range(half):
            R[base + j, base + half + j] = -1.0
            R[base + half + j, base + j] = 1.0
    return cos, sin, R


BIG_W = (
    [(f"w_mod0_{j}", 4096, 4096) for j in range(6)] +
    [("w_in_even", 4096, EVEN_IN), ("w_pool", 1024, 256), ("w_out_even", 4096, 4096),
     ("w_up0_g", 4096, FF), ("w_up0_v", 4096, FF), ("w_down0", FF, 4096)] +
    [(f"w_mod1_{j}", 4096, 4096) for j in range(6)] +
    [("w_in_odd", 4096, ODD_IN), ("w_uq", 1024, 3072), ("w_uk", 512, 2048),
     ("w_uv", 512, 2048), ("w_out_odd", 4096, 4096), ("w_up1_g", 4096, FF), ("w_up1_v", 4096, FF), ("w_down1", FF, 4096)]
)

SMALL_IN = [
    ("xs", [2048, D]), ("xp", [512, D]),
    ("ca_k", [256, 768]), ("ca_v", [256, 768]), ("cc_ckv", [256, 512]), ("cc_kr", [256, 64]),
    ("cd_k", [256, 2048]), ("cd_v", [256, 2048]),
    ("condT", [128, KC, 2]), ("b_modT", [2, 128, 192]), ("g4T", [2, 4, 128, KC]),
    ("conv_wT", [2, 3, 128, 172]), ("conv_bT", [2, 128, 172]),
    ("a_sink", [1, 24]), ("pool_scaleT", [128, 8]), ("g_qnT", [128, 8]), ("g_kvnT", [128, 4]),
    ("g_sublnT", [128, 2]), ("lams", [4, 128]),
    ("cos128", [128, 2048]), ("sin128", [128, 2048]), ("cos64", [64, 2048]), ("sin64", [64, 2048]),
    ("rot128T", [128, 128]), ("rot64T", [64, 64]), ("ident", [128, 128]),
    ("mask_ge", [128, 128]), ("mask_le", [128, 128]), ("invcnt", [4, 2592]), ("sel", [1, 8]),
]

FULL_W = [("w_mod", [2, 4096, 24576]), ("w_in_even", [1, 4096, EVEN_IN]), ("w_out_even", [1, 4096, 4096]),
          ("w_pool", [1, 4, 256, 256]), ("w_in_odd", [1, 4096, ODD_IN]), ("w_out_odd", [1, 4096, 4096]),
          ("w_uq", [1, 1024, 3072]), ("w_uk", [1, 512, 16, 128]), ("w_uv", [1, 512, 16, 128]),
          ("w_up", [2, 4096, 2 * FF]), ("w_down", [2, FF, 4096])]

OUTS = [("y_s", [2048, D]), ("y_p", [512, D]), ("o_ak", [2, 256, 768]), ("o_av", [2, 256, 768]),
        ("o_ckv", [2, 256, 512]), ("o_kr", [2, 256, 64]), ("o_dk", [2, 256, 2048]), ("o_dv", [2, 256, 2048])]


LAM_INIT = 0.8 - 0.6 * math.exp(-0.3 * 1)


def build_program(stop_after=None):
    nc = bass.Bass("TRN2", target_bir_lowering=False)
    I = {}
    for name, shape in SMALL_IN:
        I[name] = nc.dram_tensor(name, list(shape), F32, kind="ExternalInput").ap()
    O = {}
    for name, shape in OUTS:
        O[name] = nc.dram_tensor(name, list(shape), F32, kind="ExternalOutput").ap()
    WS, WB, W = {}, {}, {}
    for name, K, N in BIG_W:
        if SHARD_WEIGHTS:
            WS[name] = nc.dram_tensor(name, [K // NCORES, N], F32, kind="ExternalInput").ap()
            WB[name] = nc.dram_tensor(name + "_b", [K, N], F32).ap()
            W[name] = nc.dram_tensor(name + "_f", [K, N], F32).ap()
    if DEBUG_NOW:
        for name, K, N in BIG_W:
            W[name] = nc.dram_tensor(name + "_dbg", [K, N], F32).ap()
    elif not SHARD_WEIGHTS:
        FW = {}
        for name, shape in FULL_W:
            FW[name] = nc.dram_tensor(name, list(shape), F32, kind="ExternalInput").ap()
        for l in range(2):
            for j in range(6):
                W[f"w_mod{l}_{j}"] = FW["w_mod"][l][:, j * 4096:(j + 1) * 4096]
            W[f"w_up{l}_g"] = FW["w_up"][l][:, 0:FF]
            W[f"w_up{l}_v"] = FW["w_up"][l][:, FF:2 * FF]
            W[f"w_down{l}"] = FW["w_down"][l]
        W["w_in_even"] = FW["w_in_even"][0]
        W["w_out_even"] = FW["w_out_even"][0]
        W["w_pool"] = FW["w_pool"][0].rearrange("g c d -> (g c) d")
        W["w_in_odd"] = FW["w_in_odd"][0]
        W["w_out_odd"] = FW["w_out_odd"][0]
        W["w_uq"] = FW["w_uq"][0]
        W["w_uk"] = FW["w_uk"][0].rearrange("l h d -> l (h d)")
        W["w_uv"] = FW["w_uv"][0].rearrange("l h d -> l (h d)")

    def scr(name, shape, dt):
        return nc.dram_tensor(name, list(shape), dt).ap()

    XT = scr("XT", [D, T], F32)
    YT = scr("YT", [D, T], F32)
    RS = scr("RS", [128, T], F32)
    MIXT = scr("MIXT", [D, T], BF16)
    UT = scr("UT", [FF, T], BF16)
    QT0 = scr("QT0", [3072, T], BF16)
    KT0 = scr("KT0", [768, T], BF16)
    VT0 = scr("VT0", [768, T], F32)
    UP0 = scr("UP0", [1024, T], F32)
    CQT = scr("CQT", [1024, T], F32)
    CKVT = scr("CKVT", [512, T], F32)
    KRR = scr("KRR", [64, T + 256], BF16)
    DQT = scr("DQT", [2048, T], BF16)
    DKT = scr("DKT", [2048, T], BF16)
    DVT = scr("DVT", [2048, T], F32)
    KNT = scr("KNT", [2048, T + 256], BF16)
    VTOK = scr("VTOK", [T + 256, 2048], BF16)
    QNT = scr("QNT", [2048, T], BF16)
    QRT = scr("QRT", [1024, T], BF16)

    P = Prog(nc)
    es = ExitStack()
    ARW = 49152 - 256
    arena_t = es.enter_context(nc.sbuf_tensor("arena", [128, ARW], F32))
    A = Arena(arena_t[:, :], ARW)
    psum = es.enter_context(nc.psum_tensor("psum", [128, 8, 512], F32))
    pb = [Buf(f"ps{i}") for i in range(8)]

    wbuf = {name: Buf(ro=True) for name, K, N in BIG_W}

    ident = A.alloc([128, 128]); ones_f = A.alloc([128, 128]); ones_b = A.alloc([128, 128], BF16)
    rot128T = A.alloc([128, 128]); rot64T = A.alloc([64, 64])
    mask_ge = A.alloc([128, 128], BF16); mask_le = A.alloc([128, 128], BF16)
    mtmp = A.alloc([128, 256])
    modsT = A.alloc([128, 192, 2])
    AB = {k: A.alloc([128, KC, 2]) for k in ("A1", "B1", "A2", "A3", "B3", "A4", "A4p")}
    g4 = A.alloc([128, 4, KC])
    bmod = A.alloc([128, 192])
    condT = A.alloc([128, KC, 2]); scb = A.alloc([128, KC, 2], BF16)
    esink = A.alloc([128, 24])
    neglam = A.alloc([128, 1]); lamt = A.alloc([128, 4, 128]); lsum = A.alloc([128, 4])
    bconst = Buf("const")
    bmods = Buf("mods")
    bAB = Buf("AB")

    P.dma("sp", ident, I["ident"], [], [bconst])
    P.dma("sp", rot128T, I["rot128T"], [], [bconst])
    P.dma("sp", rot64T, I["rot64T"], [], [bconst])
    P.dma("sp", mtmp[:, 0:128], I["mask_ge"], [], [bconst])
    P.dma("sp", mtmp[:, 128:256], I["mask_le"], [], [bconst])
    P.dma("sp", condT, I["condT"], [], [bconst])
    P.dma("sp", esink, I["a_sink"].partition_broadcast(128), [], [bconst])
    P.dma("sp", lamt, I["lams"].partition_broadcast(128), [], [bconst])
    P.op("pool", lambda e: e.memset(ones_f, 1.0), [], [bconst])
    P.op("pool", lambda e: e.memset(ones_b, 1.0), [], [bconst])
    P.op("dve", lambda e: e.tensor_copy(out=mask_ge, in_=mtmp[:, 0:128]), [bconst], [bconst])
    P.op("dve", lambda e: e.tensor_copy(out=mask_le, in_=mtmp[:, 128:256]), [bconst], [bconst])
    P.op("act", lambda e: e.activation(out=scb, in_=condT, func=AF.Silu), [bconst], [bconst])
    P.op("act", lambda e: e.activation(out=esink, in_=esink, func=AF.Exp), [bconst], [bconst])
    P.op("dve", lambda e: e.tensor_tensor(out=lamt[:, 0, :], in0=lamt[:, 0, :], in1=lamt[:, 1, :], op=ALU.mult), [bconst], [bconst])
    P.op("dve", lambda e: e.tensor_tensor(out=lamt[:, 2, :], in0=lamt[:, 2, :], in1=lamt[:, 3, :], op=ALU.mult), [bconst], [bconst])
    P.op("dve", lambda e: e.reduce_sum(out=lsum[:, 0:1], in_=lamt[:, 0, :], axis=AX.X), [bconst], [bconst])
    P.op("dve", lambda e: e.reduce_sum(out=lsum[:, 1:2], in_=lamt[:, 2, :], axis=AX.X), [bconst], [bconst])
    P.op("act", lambda e: e.activation(out=lsum[:, 2:4], in_=lsum[:, 0:2], func=AF.Exp), [bconst], [bconst])
    P.op("dve", lambda e: e.tensor_tensor(out=neglam, in0=lsum[:, 3:4], in1=lsum[:, 2:3], op=ALU.subtract), [bconst], [bconst])
    P.op("dve", lambda e: e.tensor_scalar(out=neglam, in0=neglam, scalar1=-LAM_INIT, scalar2=None, op0=ALU.add), [bconst], [bconst])
    A.mark()

    bank_rr = [0]
    A_gen = [0]

    def linear(xb, bx, kc, colchunks, toktiles, epi, banks, kparts=1):
        wrow0 = 0
        kcp = kc // kparts
        wf = [A.alloc([128, kcp, 128]) for _ in range(2)]
        wb = [A.alloc([128, kcp, 128], BF16) for _ in range(2)]
        bwf = [Buf() for _ in range(2)]
        bwb = [Buf() for _ in range(2)]
        cnt = 0
        for ci, (wname, c0, ncl) in enumerate(colchunks):
            wsrc = W[wname]
            tb = []
            for kp in range(kparts):
                s = cnt % 2
                cnt += 1
                r0 = wrow0 + kp * kcp * 128
                P.dma("sp", wf[s][:, :, 0:ncl],
                      wsrc[r0:r0 + kcp * 128, c0:c0 + ncl].rearrange("(c p) n -> p c n", p=128),
                      [wbuf[wname]], [bwf[s]])
                ceng = "dve" if (cnt % 2 == 0) else "pool"
                P.op(ceng, lambda e, s=s, ncl=ncl: e.tensor_copy(out=wb[s][:, :, 0:ncl], in_=wf[s][:, :, 0:ncl]),
                     [bwf[s]], [bwb[s]])
                for ti, (t0, n) in enumerate(toktiles):
                    if kp == 0:
                        bk = banks[bank_rr[0] % len(banks)]
                        bank_rr[0] += 1
                        tb.append(bk)
                    bk = tb[ti]
                    for k in range(kcp):
                        kk = kp * kcp + k
                        P.op("pe", lambda e, s=s, k=k, kk=kk, bk=bk, t0=t0, n=n, ncl=ncl: e.matmul(
                            psum[0:ncl, bk, 0:n], lhsT=wb[s][:, k, 0:ncl], rhs=xb[:, kk, t0:t0 + n],
                            start=(kk == 0), stop=(kk == kc - 1)), [bwb[s], bx], [pb[bk]])
                    if kp == kparts - 1:
                        epi(ci, c0, ncl, ti, t0, n, psum[0:ncl, bk, 0:n], pb[bk])

    tmo = {}

    def tok_major_out(src, bsrc, nf, dst, bank):
        if tmo.get("off") != A.base or "ring" not in tmo or tmo["gen"] != A_gen[0]:
            tmo["ring"] = [(A.alloc([128, 2, 128]), Buf()) for _ in range(2)]
            tmo["gen"] = A_gen[0]
            tmo["off"] = A.base
            tmo["i"] = 0
        stg, bst = tmo["ring"][tmo["i"] % 2]
        tmo["i"] += 1
        for j in range(2):
            P.op("pe", lambda e, j=j: e.matmul(psum[:, bank, j * 128:j * 128 + nf], lhsT=src[0:nf, j * 128:(j + 1) * 128], rhs=ident[0:nf, 0:nf], start=True, stop=True),
                 [bsrc, bconst], [pb[bank]])
        if DBG_CUT <= 4 or DBG_CUT in (41, 42):
            return
        P.op("act", lambda e: e.activation(out=stg[:, :, 0:nf],
                                           in_=psum[:, bank, 0:256].rearrange("p (j f) -> p j f", j=2)[:, :, 0:nf],
                                           func=AF.Copy), [pb[bank]], [bst])
        if DBG_CUT <= 5:
            return
        P.dma("pool", dst.rearrange("(j p) f -> p j f", p=128), stg[:, :, 0:nf], [bst], [])

    def rms_rstd(src3, bsrc, nch, n, dim, bank, out_rstd, brs, sq=None):
        if sq is None:
            sq = A.alloc([128, nch, n])
        bsq = Buf()
        ssp = A.alloc([128, n])
        bss = Buf()
        P.op("act", lambda e: e.activation(out=sq, in_=src3, func=AF.Square), [bsrc], [bsq])
        if nch > 1:
            P.op("dve", lambda e: e.tensor_reduce(out=ssp, in_=sq.rearrange("p c t -> p t c"), axis=AX.X, op=ALU.add),
                 [bsq], [bss])
            red = ssp
        else:
            red = sq[:, 0, :]
            bss = bsq
        P.op("pe", lambda e: e.matmul(psum[:, bank, 0:n], lhsT=ones_f, rhs=red, start=True, stop=True),
             [bss, bconst], [pb[bank]])
        P.op("dve", lambda e: e.tensor_scalar(out=out_rstd, in0=psum[:, bank, 0:n], scalar1=1.0 / dim, scalar2=EPS,
                                              op0=ALU.mult, op1=ALU.add), [pb[bank]], [brs])
        P.op("act", lambda e: e.activation(out=out_rstd, in_=out_rstd, func=AF.Sqrt), [brs], [brs])
        P.op("dve", lambda e: e.reciprocal(out=out_rstd, in_=out_rstd), [brs], [brs])
        return sq, bsq

    def grp_of(g0):
        return 1 if (g0 % 1280) >= 1024 else 0

    def make_h(hT, bh, ranges, Ak, Bk, fuse_key=None, bank=7):
        NB = 2
        xt = [A.alloc([128, KC, 128]) for _ in range(NB)]
        yt = [A.alloc([128, KC, 128]) for _ in range(NB)]
        rsb = [A.alloc([128, 128]) for _ in range(NB)]
        rstd = [A.alloc([128, 128]) for _ in range(NB)]
        ssp = [A.alloc([128, 128]) for _ in range(NB)]
        bxt = [Buf() for _ in range(NB)]
        byt = [Buf() for _ in range(NB)]
        brsb = [Buf() for _ in range(NB)]
        brstd = [Buf() for _ in range(NB)]
        bssp = [Buf() for _ in range(NB)]
        for i, (g0, n, lcol, fuse, wb_) in enumerate(ranges):
            s = i % NB
            g = grp_of(g0)
            x_, y_, r_, rs_, ss_ = xt[s][:, :, 0:n], yt[s][:, :, 0:n], rsb[s][:, 0:n], rstd[s][:, 0:n], ssp[s][:, 0:n]
            P.dma("sp", x_, XT[:, g0:g0 + n].rearrange("(c p) t -> p c t", p=128), [], [bxt[s]])
            if fuse and fuse_key is not None:
                P.dma("sp", y_, YT[:, g0:g0 + n].rearrange("(c p) t -> p c t", p=128), [], [byt[s]])
                P.dma("sp", r_, RS[:, g0:g0 + n], [], [brsb[s]])
                P.op("dve", lambda e, y_=y_, r_=r_, n=n: e.tensor_tensor(
                    out=y_, in0=y_, in1=r_.unsqueeze(1).to_broadcast([128, KC, n]), op=ALU.mult), [byt[s], brsb[s]], [byt[s]])
                P.op("pool", lambda e, y_=y_, n=n, g=g: e.tensor_tensor(
                    out=y_, in0=y_, in1=AB[fuse_key][:, :, g:g + 1].to_broadcast([128, KC, n]), op=ALU.mult),
                    [byt[s], bAB], [byt[s]])
                P.op("dve", lambda e, x_=x_, y_=y_: e.tensor_tensor(out=x_, in0=x_, in1=y_, op=ALU.add),
                     [byt[s], bxt[s]], [bxt[s]])
                if wb_:
                    P.dma("pool", XT[:, g0:g0 + n].rearrange("(c p) t -> p c t", p=128), x_, [bxt[s]], [])
            if hT is None:
                continue
            P.op("act", lambda e, x_=x_, y_=y_: e.activation(out=y_, in_=x_, func=AF.Square), [bxt[s]], [byt[s]])
            P.op("dve", lambda e, y_=y_, ss_=ss_: e.tensor_reduce(out=ss_, in_=y_.rearrange("p c t -> p t c"),
                                                                 axis=AX.X, op=ALU.add), [byt[s]], [bssp[s]])
            P.op("pe", lambda e, ss_=ss_, n=n: e.matmul(psum[:, bank, 0:n], lhsT=ones_f, rhs=ss_, start=True, stop=True),
                 [bssp[s], bconst], [pb[bank]])
            P.op("dve", lambda e, rs_=rs_, n=n: e.tensor_scalar(out=rs_, in0=psum[:, bank, 0:n], scalar1=1.0 / D,
                                                               scalar2=EPS, op0=ALU.mult, op1=ALU.add), [pb[bank]], [brstd[s]])
            P.op("act", lambda e, rs_=rs_: e.activation(out=rs_, in_=rs_, func=AF.Sqrt), [brstd[s]], [brstd[s]])
            P.op("dve", lambda e, rs_=rs_: e.reciprocal(out=rs_, in_=rs_), [brstd[s]], [brstd[s]])
            P.op("dve", lambda e, x_=x_, y_=y_, rs_=rs_, n=n: e.tensor_tensor(
                out=y_, in0=x_, in1=rs_.unsqueeze(1).to_broadcast([128, KC, n]), op=ALU.mult),
                [bxt[s], brstd[s], byt[s]], [byt[s]])
            for c in range(KC):
                P.op("act", lambda e, c=c, y_=y_, lcol=lcol, n=n, g=g: e.activation(
                    out=hT[:, c, lcol:lcol + n], in_=y_[:, c, :], func=AF.Identity,
                    bias=AB[Bk][:, c, g:g + 1], scale=AB[Ak][:, c, g:g + 1]), [byt[s], bAB], [bh])

    def split128(g0, n, lcol, fuse=True, wb_=True):
        return [(g0 + o, min(128, n - o), lcol + o, fuse, wb_) for o in range(0, n, 128)]

    nphase = [0]

    def phase_end():
        P.barrier()
        A.reset()
        A_gen[0] += 1
        nphase[0] += 1
        if STOP_PHASE is not None and nphase[0] >= STOP_PHASE:
            raise StopIteration

    def phase_load_x():
        xin = [A.alloc([128, D]) for _ in range(2)]
        xo = [A.alloc([128, KC, 128]) for _ in range(2)]
        bi = [Buf() for _ in range(2)]
        bo = [Buf() for _ in range(2)]
        tiles = []
        for b in range(16):
            tiles.append((I["xs"][b * 128:(b + 1) * 128, :], sgs(b * 128)))
        for q in range(2):
            for j in range(2):
                tiles.append((I["xp"][q * 256 + j * 128:q * 256 + (j + 1) * 128, :], 1024 + q * 1280 + j * 128))
        bkr = 0
        for i, (src, g0) in enumerate(tiles):
            s = i % 2
            P.dma("sp", xin[s], src, [], [bi[s]])
            for c4 in range(8):
                bk = bkr % 8
                bkr += 1
                for j in range(4):
                    c = c4 * 4 + j
                    P.op("pe", lambda e, s=s, c=c, bk=bk, j=j: e.matmul(psum[:, bk, j * 128:(j + 1) * 128], lhsT=xin[s][:, c * 128:(c + 1) * 128], rhs=ident, start=True, stop=True), [bi[s], bconst], [pb[bk]])
                eng = "act" if c4 % 2 == 0 else "dve"
                if eng == "act":
                    P.op("act", lambda e, s=s, c4=c4, bk=bk: e.activation(
                        out=xo[s][:, c4 * 4:(c4 + 1) * 4, :], in_=psum[:, bk, :].rearrange("p (j t) -> p j t", j=4),
                        func=AF.Copy), [pb[bk]], [bo[s]])
                else:
                    P.op("dve", lambda e, s=s, c4=c4, bk=bk: e.tensor_copy(
                        out=xo[s][:, c4 * 4:(c4 + 1) * 4, :], in_=psum[:, bk, :].rearrange("p (j t) -> p j t", j=4)),
                        [pb[bk]], [bo[s]])
            P.dma("pool", XT[:, g0:g0 + 128].rearrange("(c p) t -> p c t", p=128), xo[s], [bo[s]], [])
        phase_end()

    def phase_mods(layer):
        P.dma("sp", bmod, I["b_modT"][layer], [], [bmods])
        P.dma("sp", g4, I["g4T"][layer].rearrange("k p c -> p k c"), [], [bmods])

        def epi(ci, c0, ncl, ti, t0, n, ps, bps):
            P.op("act", lambda e: e.activation(out=modsT[:, ci, :], in_=ps, func=AF.Identity, bias=bmod[:, ci:ci + 1],
                                               scale=1.0), [bps, bmods], [bmods])
        linear(scb, bconst, KC, [(f"w_mod{layer}_{c // 32}", (c % 32) * 128, 128) for c in range(192)], [(0, 2)], epi,
               [0, 1, 2, 3])
        if layer > 0:
            P.op("dve", lambda e: e.tensor_copy(out=AB["A4p"], in_=AB["A4"]), [bAB], [bAB])
        for g in range(2):
            def m(j, g=g):
                return modsT[:, j * 32:(j + 1) * 32, g]
            P.op("dve", lambda e, g=g, m=m: e.scalar_tensor_tensor(out=AB["A1"][:, :, g], in0=m(1), scalar=1.0, in1=g4[:, 0, :],
                                                              op0=ALU.add, op1=ALU.mult), [bmods], [bAB])
            P.op("dve", lambda e, g=g, m=m: e.tensor_copy(out=AB["B1"][:, :, g], in_=m(0)), [bmods], [bAB])
            P.op("dve", lambda e, g=g, m=m: e.tensor_tensor(out=AB["A2"][:, :, g], in0=m(2), in1=g4[:, 1, :], op=ALU.mult),
                 [bmods], [bAB])
            P.op("dve", lambda e, g=g, m=m: e.scalar_tensor_tensor(out=AB["A3"][:, :, g], in0=m(4), scalar=1.0, in1=g4[:, 2, :],
                                                              op0=ALU.add, op1=ALU.mult), [bmods], [bAB])
            P.op("dve", lambda e, g=g, m=m: e.tensor_copy(out=AB["B3"][:, :, g], in_=m(3)), [bmods], [bAB])
            P.op("dve", lambda e, g=g, m=m: e.tensor_tensor(out=AB["A4"][:, :, g], in0=m(5), in1=g4[:, 3, :], op=ALU.mult),
                 [bmods], [bAB])
        phase_end()

    def phase_gather():
        if not (SHARD_WEIGHTS and not DEBUG_NOW):
            return
        selt = A.alloc([128, 8]); bsel = Buf()
        P.dma("sp", selt, I["sel"].partition_broadcast(128), [], [bsel])
        tin = [(A.alloc([128, 4096]), Buf()) for _ in range(2)]
        tout = [(A.alloc([128, 4096]), Buf()) for _ in range(4)]
        ti_ = 0
        to_ = 0
        for name, K, N in BIG_W:
            rows = K // NCORES
            bws = []
            for r0 in range(0, rows, 128):
                nr = min(128, rows - r0)
                for c0 in range(0, N, 4096):
                    ncw = min(4096, N - c0)
                    t_, bt_ = tin[ti_ % 2]
                    ti_ += 1
                    P.dma("sp", t_[0:nr, 0:ncw], WS[name][r0:r0 + nr, c0:c0 + ncw], [], [bt_])
                    for r in range(NCORES):
                        o_, bo_ = tout[to_ % 4]
                        if to_ % 2 == 0:
                            P.op("act", lambda e, o_=o_, t_=t_, nr=nr, ncw=ncw, r=r: e.activation(
                                out=o_[0:nr, 0:ncw], in_=t_[0:nr, 0:ncw], func=AF.Copy, scale=selt[0:nr, r:r + 1]),
                                [bt_, bsel], [bo_])
                        else:
                            P.op("dve", lambda e, o_=o_, t_=t_, nr=nr, ncw=ncw, r=r: e.tensor_scalar(
                                out=o_[0:nr, 0:ncw], in0=t_[0:nr, 0:ncw], scalar1=selt[0:nr, r:r + 1], scalar2=None,
                                op0=ALU.mult), [bt_, bsel], [bo_])
                        P.dma("sp" if to_ % 2 == 0 else "pool",
                              WB[name][r * rows + r0:r * rows + r0 + nr, c0:c0 + ncw], o_[0:nr, 0:ncw], [bo_], [Buf(ro=True)])
                        bws.append((P.ops["sp" if to_ % 2 == 0 else "pool"][-1]))
                        to_ += 1
            b1 = wbuf[name]
            P.op("pool", lambda e, name=name: e.collective_compute(
                "AllReduce", ALU.add, replica_groups=[list(range(NCORES))],
                ins=[WB[name].opt()], outs=[W[name].opt()]), [], [b1], is_dma=True, q="cc", inc=1)
            P.ops["pool"][-1].deps.extend(bws)

        P.barrier()
        A.reset()
        A_gen[0] += 1

    def sub_reset(mark):
        P.barrier()
        A.off = mark
        A_gen[0] += 1

    def phase_inproj(layer, p):
        g_base = p * 1280
        hT = A.alloc([128, KC, 1280], BF16)
        bh = Buf()
        mk = A.off
        fk = "A4p" if layer > 0 else None
        make_h(hT, bh, split128(g_base, 1280, 0, fuse=True, wb_=True), "A1", "B1", fuse_key=fk)
        sub_reset(mk)
        if DBG_CUT <= 1:
            phase_end()
            return
        s_base = p * 1024
        cos = A.alloc([128, 1024]); sin = A.alloc([128, 1024])
        btab = Buf()
        P.dma("sp", cos, I["cos128"][:, s_base:s_base + 1024], [], [btab])
        P.dma("sp", sin, I["sin128"][:, s_base:s_base + 1024], [], [btab])
        if layer == 1:
            cos6 = A.alloc([64, 1024]); sin6 = A.alloc([64, 1024])
            P.dma("sp", cos6, I["cos64"][:, s_base:s_base + 1024], [], [btab])
            P.dma("sp", sin6, I["sin64"][:, s_base:s_base + 1024], [], [btab])
        stg16 = [(A.alloc([128, 1280], BF16), Buf()) for _ in range(2)]
        stg32 = [(A.alloc([128, 1280]), Buf()) for _ in range(2)]
        xsr = [(A.alloc([128, 512]), Buf()) for _ in range(2)]
        t1r = [(A.alloc([128, 512]), Buf()) for _ in range(2)]
        kfr = [(A.alloc([128, 256]), Buf()) for _ in range(2)]
        cnt = {"x": 0, "k": 0, "s16": 0, "s32": 0, "rb": 0}
        toktiles = [(0, 512), (512, 512), (1024, 256)]

        def rope(ps, bps, nf, t0, n, out, bout, c_, s_, rT):
            xs, bxs = xsr[cnt["x"] % 2]
            t1, bt1 = t1r[cnt["x"] % 2]
            cnt["x"] += 1
            bk2 = 4 + cnt["rb"] % 2
            cnt["rb"] += 1
            P.op("act", lambda e: e.activation(out=xs[0:nf, 0:n], in_=ps, func=AF.Copy), [bps], [bxs])
            P.op("pe", lambda e: e.matmul(psum[0:nf, bk2, 0:n], lhsT=rT, rhs=xs[0:nf, 0:n], start=True, stop=True),
                 [bxs, bconst], [pb[bk2]])
            P.op("dve", lambda e: e.tensor_tensor(out=t1[0:nf, 0:n], in0=psum[0:nf, bk2, 0:n], in1=s_[0:nf, t0:t0 + n],
                                                  op=ALU.mult), [pb[bk2], btab], [bt1])
            P.op("pool", lambda e: e.tensor_tensor(out=xs[0:nf, 0:n], in0=xs[0:nf, 0:n], in1=c_[0:nf, t0:t0 + n],
                                                   op=ALU.mult), [bxs, btab], [bxs])
            P.op("dve", lambda e: e.tensor_tensor(out=out, in0=xs[0:nf, 0:n], in1=t1[0:nf, 0:n], op=ALU.add),
                 [bxs, bt1], [bout])

        def plain(ps, bps, out, bout, eng="act"):
            if eng == "act":
                P.op("act", lambda e: e.activation(out=out, in_=ps, func=AF.Copy), [bps], [bout])
            else:
                P.op("dve", lambda e: e.tensor_copy(out=out, in_=ps), [bps], [bout])

        def tokmaj_from_ps(ps, bps, nf, dst):
            kf, bkf = kfr[cnt["k"] % 2]
            cnt["k"] += 1
            P.op("dve", lambda e: e.tensor_copy(out=kf[0:nf, :], in_=ps), [bps], [bkf])
            tok_major_out(kf, bkf, nf, dst, 6)

        chunks = []
        if layer == 0:
            for h in range(24):
                chunks.append((h * 128, 128, QT0[h * 128:(h + 1) * 128, :], True, False, None))
            for h in range(6):
                chunks.append((3072 + h * 128, 128, KT0[h * 128:(h + 1) * 128, :], True, False,
                               O["o_ak"][p, :, h * 128:(h + 1) * 128]))
            for h in range(6):
                chunks.append((3840 + h * 128, 128, VT0[h * 128:(h + 1) * 128, :], False, True,
                               O["o_av"][p, :, h * 128:(h + 1) * 128]))
            for c in range(8):
                chunks.append((4608 + c * 128, 128, UP0[c * 128:(c + 1) * 128, :], False, True, None))
        else:
            for c in range(8):
                chunks.append((c * 128, 128, CQT[c * 128:(c + 1) * 128, :], False, True, None))
            for c in range(4):
                chunks.append((1024 + c * 128, 128, CKVT[c * 128:(c + 1) * 128, :], False, True, None))
            chunks.append((1536, 64, KRR[:, 0:T], True, False, O["o_kr"][p, :, :]))
            for c in range(16):
                chunks.append((1600 + c * 128, 128, DQT[c * 128:(c + 1) * 128, :], True, False, None))
            for c in range(16):
                chunks.append((3648 + c * 128, 128, DKT[c * 128:(c + 1) * 128, :], True, False,
                               O["o_dk"][p, :, c * 128:(c + 1) * 128]))
            for c in range(16):
                chunks.append((5696 + c * 128, 128, DVT[c * 128:(c + 1) * 128, :], False, True,
                               O["o_dv"][p, :, c * 128:(c + 1) * 128]))
        cur = {}

        def epi(ci, c0, ncl, ti, t0, n, ps, bps):
            _, nf, dst, do_rope, is32, otm = chunks[ci]
            if ti == 0:
                if is32:
                    cur["stg"], cur["b"] = stg32[cnt["s32"] % 2]
                    cnt["s32"] += 1
                else:
                    cur["stg"], cur["b"] = stg16[cnt["s16"] % 2]
                    cnt["s16"] += 1
            stg, bst = cur["stg"], cur["b"]
            out = stg[0:nf, t0:t0 + n]
            if ti < 2 and do_rope and DBG_CUT >= 3:
                if nf == 128:
                    rope(ps, bps, nf, t0, n, out, bst, cos, sin, rot128T)
                else:
                    rope(ps, bps, nf, t0, n, out, bst, cos6, sin6, rot64T)
            elif ti == 2 and otm is not None and not is32:
                kf, bkf = kfr[cnt["k"] % 2]
                cnt["k"] += 1
                P.op("act", lambda e: e.activation(out=kf[0:nf, :], in_=ps, func=AF.Copy), [bps], [bkf])
                P.op("dve", lambda e: e.tensor_copy(out=out, in_=kf[0:nf, :]), [bkf], [bst])
                tok_major_out(kf, bkf, nf, otm, 6)
            else:
                plain(ps, bps, out, bst, "act" if (ci + ti) % 2 == 0 else "dve")
            if ti == 2 and otm is not None and is32:
                tok_major_out(stg[:, 1024:1280], bst, nf, otm, 6)
            if ti == 2:
                P.dma("pool", dst[:, g_base:g_base + 1280], stg[0:nf, :], [bst], [])

        wname = "w_in_even" if layer == 0 else "w_in_odd"
        linear(hT, bh, KC, [(wname, c[0], c[1]) for c in chunks], toktiles, epi, [0, 1, 2, 3])
        phase_end()

    def phase_attn_even():
        scale = 128 ** -0.5
        cak = A.alloc([128, 2, 768]); cav = A.alloc([128, 2, 768])
        bck, bcv = Buf(), Buf()
        P.dma("sp", cak, I["ca_k"].rearrange("(j p) f -> p j f", p=128), [], [bck])
        P.dma("sp", cav, I["ca_v"].rearrange("(j p) f -> p j f", p=128), [], [bcv])
        kctx = A.alloc([128, 6, 256], BF16); vctx = A.alloc([128, 2, 768], BF16)
        bkc, bvc = Buf(), Buf()
        P.op("dve", lambda e: e.tensor_copy(out=vctx, in_=cav), [bcv], [bvc])
        for h in range(6):
            for j in range(2):
                P.op("pe", lambda e, h=h, j=j: e.matmul(psum[:, 7, j * 128:(j + 1) * 128], lhsT=cak[:, j, h * 128:(h + 1) * 128], rhs=ident, start=True, stop=True), [bck, bconst], [pb[7]])
            P.op("act", lambda e, h=h: e.activation(out=kctx[:, h, :], in_=psum[:, 7, 0:256], func=AF.Copy), [pb[7]], [bkc])
        kT = [(A.alloc([128, T], BF16), Buf()) for _ in range(2)]
        vT = [(A.alloc([128, T]), Buf()) for _ in range(2)]
        vtok = [(A.alloc([128, 20, 128], BF16), Buf()) for _ in range(2)]
        qT = [(A.alloc([128, 4, T], BF16), Buf()) for _ in range(2)]
        om = [(A.alloc([128, 4, T], BF16), Buf()) for _ in range(2)]
        Er = [(A.alloc([128, 4, 128], BF16), Buf()) for _ in range(3)]
        dn = [(A.alloc([128, 4, 128]), Buf()) for _ in range(2)]
        ecnt = 0
        blk = 0
        for g in range(6):
            s = g % 2
            k_, bk_ = kT[s]; v_, bv_ = vT[s]; vt_, bvt_ = vtok[s]; q_, bq_ = qT[s]; o_, bo_ = om[s]
            P.dma("sp", k_, KT0[g * 128:(g + 1) * 128, :], [], [bk_])
            P.dma("sp", v_, VT0[g * 128:(g + 1) * 128, :], [], [bv_])
            P.dma("sp", q_, QT0[g * 512:(g + 1) * 512, :].rearrange("(h p) t -> p h t", p=128), [], [bq_])
            for b4 in range(5):
                for j in range(4):
                    P.op("pe", lambda e, v_=v_, b4=b4, j=j: e.matmul(psum[:, 7, j * 128:(j + 1) * 128], lhsT=v_[:, (b4 * 4 + j) * 128:(b4 * 4 + j + 1) * 128], rhs=ident, start=True, stop=True),
                        [bv_, bconst], [pb[7]])
                P.op("act", lambda e, vt_=vt_, b4=b4: e.activation(
                    out=vt_[:, b4 * 4:(b4 + 1) * 4, :], in_=psum[:, 7, :].rearrange("p (j d) -> p j d", j=4), func=AF.Copy),
                    [pb[7]], [bvt_])
            jobs = []
            for b in range(16):
                keys = []
                if b > 0:
                    keys.append(("lat", sgs((b - 1) * 128) // 128, mask_ge))
                keys.append(("lat", sgs(b * 128) // 128, None))
                if b < 15:
                    keys.append(("lat", sgs((b + 1) * 128) // 128, mask_le))
                keys.append(("ctx", 0, None)); keys.append(("ctx", 1, None))
                jobs.append((sgs(b * 128), keys))
            for q in range(2):
                for jb in range(2):
                    gq = 1024 + q * 1280 + jb * 128
                    jobs.append((gq, [("lat", (1024 + q * 1280) // 128, None), ("lat", (1024 + q * 1280) // 128 + 1, None)]))
            for (gq, keys) in jobs:
                ob = 3 + blk % 2
                db = 5 + blk % 2
                blk += 1
                for ki, (kind, idx, msk) in enumerate(keys):
                    sb_ = ecnt % 3
                    E_, bE_ = Er[sb_]
                    ecnt += 1
                    if kind == "lat":
                        lhs = k_[:, idx * 128:(idx + 1) * 128]; vv = vt_[:, idx, :]; rd = [bk_, bq_]; rdv = [bvt_]
                    else:
                        lhs = kctx[:, g, idx * 128:(idx + 1) * 128]; vv = vctx[:, idx, g * 128:(g + 1) * 128]
                        rd = [bkc, bq_]; rdv = [bvc]
                    P.op("pe", lambda e, lhs=lhs, q_=q_, gq=gq, sb_=sb_: e.matmul(
                        psum[:, sb_, :].rearrange("p (h t) -> p h t", h=4), lhsT=lhs, rhs=q_[:, :, gq:gq + 128],
                        start=True, stop=True), rd, [pb[sb_]])
                    P.op("act", lambda e, E_=E_, sb_=sb_: e.activation(
                        out=E_, in_=psum[:, sb_, :].rearrange("p (h t) -> p h t", h=4), func=AF.Exp, scale=scale),
                        [pb[sb_]], [bE_])
                    if msk is not None:
                        P.op("pool", lambda e, E_=E_, msk=msk: e.tensor_tensor(
                            out=E_, in0=E_, in1=msk.unsqueeze(1).to_broadcast([128, 4, 128]), op=ALU.mult),
                            [bE_, bconst], [bE_])
                    st, sp_ = (ki == 0), (ki == len(keys) - 1)
                    P.op("pe", lambda e, vv=vv, E_=E_, ob=ob, st=st, sp_=sp_: e.matmul(
                        psum[:, ob, :].rearrange("p (h t) -> p h t", h=4), lhsT=vv, rhs=E_, start=st, stop=sp_),
                        rdv + [bE_], [pb[ob]])
                    P.op("pe", lambda e, E_=E_, db=db, st=st, sp_=sp_: e.matmul(
                        psum[:, db, :].rearrange("p (h t) -> p h t", h=4), lhsT=ones_b, rhs=E_, start=st, stop=sp_),
                        [bE_, bconst], [pb[db]])
                d_, bd_ = dn[blk % 2]
                P.op("dve", lambda e, d_=d_, db=db, g=g: e.tensor_tensor(
                    out=d_, in0=psum[:, db, :].rearrange("p (h t) -> p h t", h=4),
                    in1=esink[:, g * 4:(g + 1) * 4].unsqueeze(2).to_broadcast([128, 4, 128]), op=ALU.add),
                    [pb[db], bconst], [bd_])
                P.op("dve", lambda e, d_=d_: e.reciprocal(out=d_, in_=d_), [bd_], [bd_])
                P.op("dve", lambda e, d_=d_, ob=ob, o_=o_, gq=gq: e.tensor_tensor(
                    out=o_[:, :, gq:gq + 128], in0=psum[:, ob, :].rearrange("p (h t) -> p h t", h=4), in1=d_, op=ALU.mult),
                    [pb[ob], bd_], [bo_])
            P.dma("pool", MIXT[g * 512:(g + 1) * 512, :].rearrange("(h p) t -> p h t", p=128), o_, [bo_], [])
        phase_end()

    def phase_pool():
        L = 2592
        segs = [(8, 0, 1024), (1032, 1280, 1024), (2064, 1024, 256), (2328, 2304, 256)]
        ic = A.alloc([128, 4, L]); bic = Buf()
        P.dma("sp", ic, I["invcnt"].partition_broadcast(128), [], [bic])
        wpf = A.alloc([128, 8, 256]); wpb = A.alloc([128, 8, 256], BF16); bwp = Buf()
        P.dma("sp", wpf, W["w_pool"].rearrange("(c p) n -> p c n", p=128), [wbuf["w_pool"]], [bwp])
        P.op("dve", lambda e: e.tensor_copy(out=wpb, in_=wpf), [bwp], [bwp])
        psc = A.alloc([128, 8]); bpsc = Buf()
        P.dma("sp", psc, I["pool_scaleT"], [], [bpsc])
        pooled = A.alloc([128, 8, T], BF16); bpl = Buf()
        ub = [(A.alloc([128, L]), Buf()) for _ in range(2)]
        sa = [(A.alloc([128, L]), Buf()) for _ in range(2)]
        sb2 = [(A.alloc([128, L]), Buf()) for _ in range(2)]
        for i in range(2):
            for t_, b_ in (ub[i], sa[i], sb2[i]):
                P.op("pool", lambda e, t_=t_: e.memset(t_, 0.0), [], [b_])
        for c in range(8):
            gi = c // 2
            nlev = gi + 1
            u_, bu_ = ub[c % 2]; a_, ba_ = sa[c % 2]; b2_, bb2_ = sb2[c % 2]
            for (pos, gc, n) in segs:
                P.dma("sp", u_[:, pos:pos + n], UP0[c * 128:(c + 1) * 128, gc:gc + n], [], [bu_])
            eng = "pool" if c % 2 == 0 else "dve"
            P.op(eng, lambda e, u_=u_, a_=a_: e.tensor_tensor(out=a_[:, 1:L], in0=u_[:, 0:L - 1], in1=u_[:, 1:L], op=ALU.add),
                 [bu_], [ba_])
            src, bsrc, dst, bdst = a_, ba_, b2_, bb2_
            h = 1
            for lev in range(1, nlev):
                P.op(eng, lambda e, src=src, dst=dst, h=h: e.tensor_tensor(
                    out=dst[:, h + 1:L - h], in0=src[:, 1:L - 2 * h], in1=src[:, 2 * h + 1:L], op=ALU.add), [bsrc], [bdst])
                src, bsrc, dst, bdst = dst, bdst, src, bsrc
                h *= 2
            P.op(eng, lambda e, src=src, gi=gi: e.tensor_tensor(out=src[:, 8:L - 8], in0=src[:, 8:L - 8], in1=ic[:, gi, 8:L - 8],
                                                                op=ALU.mult), [bsrc, bic], [bsrc])
            for (pos, gc, n) in segs:
                P.op(eng, lambda e, src=src, u_=u_, c=c, pos=pos, gc=gc, n=n: e.tensor_tensor(
                    out=pooled[:, c, gc:gc + n], in0=src[:, pos:pos + n], in1=u_[:, pos:pos + n], op=ALU.subtract),
                    [bsrc, bu_], [bpl])
        ostg = [(A.alloc([128, T], BF16), Buf()) for _ in range(2)]
        bkc_ = 0
        for gi in range(4):
            for nh in range(2):
                ch = gi * 2 + nh
                o_, bo_ = ostg[ch % 2]
                for tt in range(5):
                    bk = bkc_ % 4
                    bkc_ += 1
                    for kc in range(2):
                        P.op("pe", lambda e, gi=gi, nh=nh, kc=kc, tt=tt, bk=bk: e.matmul(
                            psum[:, bk, :], lhsT=wpb[:, gi * 2 + kc, nh * 128:(nh + 1) * 128],
                            rhs=pooled[:, gi * 2 + kc, tt * 512:(tt + 1) * 512], start=(kc == 0), stop=(kc == 1)),
                            [bwp, bpl], [pb[bk]])
                    P.op("act", lambda e, o_=o_, tt=tt, bk=bk, ch=ch: e.activation(
                        out=o_[:, tt * 512:(tt + 1) * 512], in_=psum[:, bk, :], func=AF.Copy, scale=psc[:, ch:ch + 1]),
                        [pb[bk], bpsc], [bo_])
                P.dma("pool", MIXT[3072 + ch * 128:3072 + (ch + 1) * 128, :], o_, [bo_], [])
        phase_end()

    def phase_back(src, kc, wname, g0, ncols, kparts):
        xb = A.alloc([128, kc, ncols], BF16); bx = Buf()
        P.dma("sp", xb, src[:, g0:g0 + ncols].rearrange("(c p) t -> p c t", p=128), [], [bx])
        toktiles = [(o, min(512, ncols - o)) for o in range(0, ncols, 512)]
        ss = A.alloc([128, ncols]); bss = Buf()
        P.op("pool", lambda e: e.memset(ss, 0.0), [], [bss])
        ystg = [(A.alloc([128, ncols]), Buf()) for _ in range(2)]
        sqr = [(A.alloc([128, 512]), Buf()) for _ in range(2)]
        cur = {}
        cnt = {"y": 0, "q": 0}

        def epi(ci, c0, ncl, ti, t0, n, ps, bps):
            if ti == 0:
                cur["y"], cur["b"] = ystg[cnt["y"] % 2]
                cnt["y"] += 1
            y_, by_ = cur["y"], cur["b"]
            sq_, bsq_ = sqr[cnt["q"] % 2]
            cnt["q"] += 1
            P.op("act", lambda e: e.activation(out=y_[:, t0:t0 + n], in_=ps, func=AF.Copy), [bps], [by_])
            P.op("act", lambda e: e.activation(out=sq_[:, 0:n], in_=ps, func=AF.Square), [bps], [bsq_])
            P.op("dve", lambda e: e.tensor_tensor(out=ss[:, t0:t0 + n], in0=ss[:, t0:t0 + n], in1=sq_[:, 0:n], op=ALU.add),
                 [bsq_, bss], [bss])
            if ti == len(toktiles) - 1:
                P.dma("pool", YT[c0:c0 + 128, g0:g0 + ncols], y_, [by_], [])

        linear(xb, bx, kc, [(wname, c * 128, 128) for c in range(KC)], toktiles, epi, [0, 1, 2, 3], kparts=kparts)
        rs = A.alloc([128, ncols]); brs = Buf()
        for (t0, n) in toktiles:
            P.op("pe", lambda e, t0=t0, n=n: e.matmul(psum[:, 7, 0:n], lhsT=ones_f, rhs=ss[:, t0:t0 + n], start=True, stop=True),
                 [bss, bconst], [pb[7]])
            P.op("dve", lambda e, t0=t0, n=n: e.tensor_scalar(out=rs[:, t0:t0 + n], in0=psum[:, 7, 0:n], scalar1=1.0 / D,
                                                             scalar2=EPS, op0=ALU.mult, op1=ALU.add), [pb[7]], [brs])
        P.op("act", lambda e: e.activation(out=rs, in_=rs, func=AF.Sqrt), [brs], [brs])
        P.op("dve", lambda e: e.reciprocal(out=rs, in_=rs), [brs], [brs])
        P.dma("pool", RS[:, g0:g0 + ncols], rs, [brs], [])
        phase_end()

    def phase_ffn_up(layer, p):
        g_base = p * 1280
        NL = 1281
        hT = A.alloc([128, KC, NL], BF16); bh = Buf()
        mk = A.off
        if p == 0:
            rng = split128(0, 1024, 0) + [(1280, 1, 1024, True, False)] + split128(1024, 256, 1025)
            s_off = 0
        else:
            rng = [(1023, 1, 0, False, False)] + split128(1280, 1024, 1) + split128(2304, 256, 1025)
            s_off = 1
        make_h(hT, bh, rng, "A3", "B3", fuse_key="A2")
        sub_reset(mk)
        cw = A.alloc([128, 3, 172]); cb = A.alloc([128, 172]); bcw = Buf()
        P.dma("sp", cw, I["conv_wT"][layer].rearrange("k p c -> p k c"), [], [bcw])
        P.dma("sp", cb, I["conv_bT"][layer], [], [bcw])
        ZL = 1284
        zb = [(A.alloc([128, ZL]), Buf()) for _ in range(4)]
        ab = [(A.alloc([128, ZL]), Buf()) for _ in range(4)]
        for z_, bz_ in zb:
            P.op("pool", lambda e, z_=z_: e.memset(z_, 0.0), [], [bz_])
        ustg = [(A.alloc([128, 1280], BF16), Buf()) for _ in range(2)]
        toktiles = [(0, 342), (342, 342), (684, 341), (1025, 256)]
        zpos = [1, 343, 685, 1027]
        chunks = []
        for j in range(86):
            chunks.append((f"w_up{layer}_g", j * 128, 128))
            chunks.append((f"w_up{layer}_v", j * 128, 128))
        cur = {}

        def epi(ci, c0, ncl, ti, t0, n, ps, bps):
            j, isval = ci // 2, ci % 2
            z_, bz_ = zb[(j % 2) * 2 + isval]
            a_, ba_ = ab[(j % 2) * 2 + isval]
            fc = j + 86 * isval
            zp = zpos[ti]
            if ti % 2 == 0:
                P.op("act", lambda e: e.activation(out=z_[:, zp:zp + n], in_=ps, func=AF.Copy), [bps], [bz_])
            else:
                P.op("dve", lambda e: e.tensor_copy(out=z_[:, zp:zp + n], in_=ps), [bps], [bz_])
            if ti != 3:
                return
            P.op("act", lambda e: e.activation(out=a_[:, 1:ZL - 1], in_=z_[:, 1:ZL - 1], func=AF.Identity,
                                               bias=cb[:, fc:fc + 1], scale=cw[:, 1, fc:fc + 1]), [bz_, bcw], [ba_])
            P.op("dve", lambda e: e.scalar_tensor_tensor(out=a_[:, 1:ZL - 1], in0=z_[:, 0:ZL - 2], scalar=cw[:, 0, fc:fc + 1],
                                                          in1=a_[:, 1:ZL - 1], op0=ALU.mult, op1=ALU.add), [bz_, bcw, ba_], [ba_])
            P.op("dve", lambda e: e.scalar_tensor_tensor(out=a_[:, 1:ZL - 1], in0=z_[:, 2:ZL], scalar=cw[:, 2, fc:fc + 1],
                                                         in1=a_[:, 1:ZL - 1], op0=ALU.mult, op1=ALU.add), [bz_, bcw, ba_], [ba_])
            if not isval:
                P.op("act", lambda e: e.activation(out=a_[:, 1:ZL - 1], in_=a_[:, 1:ZL - 1], func=AF.Silu), [ba_], [ba_])
                return
            ag_, bag_ = ab[(j % 2) * 2]
            u_, bu_ = ustg[j % 2]
            so = 1 + s_off
            P.op("dve", lambda e: e.tensor_tensor(out=u_[:, 0:1024], in0=ag_[:, so:so + 1024], in1=a_[:, so:so + 1024],
                                                  op=ALU.mult), [bag_, ba_], [bu_])
            P.op("pool", lambda e: e.tensor_tensor(out=u_[:, 1024:1280], in0=ag_[:, 1027:1283], in1=a_[:, 1027:1283],
                                                   op=ALU.mult), [bag_, ba_], [bu_])
            P.dma("pool", UT[j * 128:(j + 1) * 128, g_base:g_base + 1280], u_, [bu_], [])

        linear(hT, bh, KC, chunks, toktiles, epi, [0, 1, 2, 3, 4, 5])
        phase_end()

    def phase_final():
        NB = 2
        xt = [(A.alloc([128, KC, 128]), Buf()) for _ in range(NB)]
        yt = [(A.alloc([128, KC, 128]), Buf()) for _ in range(NB)]
        rsb = [(A.alloc([128, 128]), Buf()) for _ in range(NB)]
        ot = [(A.alloc([128, D]), Buf()) for _ in range(NB)]
        tiles = [(sgs(b * 128), O["y_s"][b * 128:(b + 1) * 128, :]) for b in range(16)]
        for q in range(2):
            for j in range(2):
                tiles.append((1024 + q * 1280 + j * 128, O["y_p"][q * 256 + j * 128:q * 256 + (j + 1) * 128, :]))
        bkr = 0
        for i, (g0, dst) in enumerate(tiles):
            s = i % NB
            g = grp_of(g0)
            x_, bx_ = xt[s]; y_, by_ = yt[s]; r_, br_ = rsb[s]; o_, bo_ = ot[s]
            P.dma("sp", x_, XT[:, g0:g0 + 128].rearrange("(c p) t -> p c t", p=128), [], [bx_])
            P.dma("sp", y_, YT[:, g0:g0 + 128].rearrange("(c p) t -> p c t", p=128), [], [by_])
            P.dma("sp", r_, RS[:, g0:g0 + 128], [], [br_])
            P.op("dve", lambda e, y_=y_, r_=r_: e.tensor_tensor(
                out=y_, in0=y_, in1=r_.unsqueeze(1).to_broadcast([128, KC, 128]), op=ALU.mult), [by_, br_], [by_])
            P.op("pool", lambda e, y_=y_, g=g: e.tensor_tensor(
                out=y_, in0=y_, in1=AB["A4"][:, :, g:g + 1].to_broadcast([128, KC, 128]), op=ALU.mult), [by_, bAB], [by_])
            P.op("dve", lambda e, x_=x_, y_=y_: e.tensor_tensor(out=x_, in0=x_, in1=y_, op=ALU.add), [by_, bx_], [bx_])
            for c4 in range(8):
                bk = bkr % 8
                bkr += 1
                for j in range(4):
                    c = c4 * 4 + j
                    P.op("pe", lambda e, x_=x_, c=c, bk=bk, j=j: e.matmul(psum[:, bk, j * 128:(j + 1) * 128], lhsT=x_[:, c, :], rhs=ident, start=True, stop=True), [bx_, bconst], [pb[bk]])
                if c4 % 2 == 0:
                    P.op("act", lambda e, o_=o_, c4=c4, bk=bk: e.activation(
                        out=o_[:, c4 * 512:(c4 + 1) * 512], in_=psum[:, bk, :], func=AF.Copy), [pb[bk]], [bo_])
                else:
                    P.op("dve", lambda e, o_=o_, c4=c4, bk=bk: e.tensor_copy(
                        out=o_[:, c4 * 512:(c4 + 1) * 512], in_=psum[:, bk, :]), [pb[bk]], [bo_])
            P.dma("pool", dst, o_, [bo_], [])
        phase_end()


    def rope_apply(ps, bps, nf, n, out, bout, cos_t, sin_t, btab, rT, xs, bxs, t1, bt1, bk2):
        P.op("act", lambda e: e.activation(out=xs[0:nf, 0:n], in_=ps, func=AF.Copy), [bps], [bxs])
        P.op("pe", lambda e: e.matmul(psum[0:nf, bk2, 0:n], lhsT=rT, rhs=xs[0:nf, 0:n], start=True, stop=True),
             [bxs, bconst], [pb[bk2]])
        P.op("dve", lambda e: e.tensor_tensor(out=t1[0:nf, 0:n], in0=psum[0:nf, bk2, 0:n], in1=sin_t, op=ALU.mult),
             [pb[bk2], btab], [bt1])
        P.op("pool", lambda e: e.tensor_tensor(out=xs[0:nf, 0:n], in0=xs[0:nf, 0:n], in1=cos_t, op=ALU.mult),
             [bxs, btab], [bxs])
        P.op("dve", lambda e: e.tensor_tensor(out=out, in0=xs[0:nf, 0:n], in1=t1[0:nf, 0:n], op=ALU.add),
             [bxs, bt1], [bout])

    def rstd_from(src3, bsrc, nch, n, dim, sq, bsq, ssp, bss, out_rstd, brs, bank=7):
        P.op("act", lambda e: e.activation(out=sq, in_=src3, func=AF.Square), [bsrc], [bsq])
        P.op("dve", lambda e: e.tensor_reduce(out=ssp, in_=sq.rearrange("p c t -> p t c"), axis=AX.X, op=ALU.add),
             [bsq], [bss])
        P.op("pe", lambda e: e.matmul(psum[:, bank, 0:n], lhsT=ones_f, rhs=ssp, start=True, stop=True),
             [bss, bconst], [pb[bank]])
        P.op("dve", lambda e: e.tensor_scalar(out=out_rstd, in0=psum[:, bank, 0:n], scalar1=1.0 / dim, scalar2=EPS,
                                              op0=ALU.mult, op1=ALU.add), [pb[bank]], [brs])
        P.op("act", lambda e: e.activation(out=out_rstd, in_=out_rstd, func=AF.Sqrt), [brs], [brs])
        P.op("dve", lambda e: e.reciprocal(out=out_rstd, in_=out_rstd), [brs], [brs])

    GT = [(0, 512, 0), (512, 512, 512), (1024, 256, -1), (1280, 512, 1024), (1792, 512, 1536), (2304, 256, -1)]

    def phase_mla_prep():
        TK = T + 256
        ck = A.alloc([128, 4, T]); bck = Buf()
        P.dma("sp", ck, CKVT.rearrange("(c p) t -> p c t", p=128), [], [bck])
        gk = A.alloc([128, 4]); gq = A.alloc([128, 8]); bg = Buf()
        P.dma("sp", gk, I["g_kvnT"], [], [bg])
        P.dma("sp", gq, I["g_qnT"], [], [bg])
        ckb = A.alloc([128, 4, TK], BF16); bckb = Buf()
        sq = A.alloc([128, 4, 512]); bsq = Buf(); ssp = A.alloc([128, 512]); bss = Buf()
        rs = A.alloc([128, 512]); brs = Buf()
        for t5 in range(5):
            sl = slice(t5 * 512, (t5 + 1) * 512)
            rstd_from(ck[:, :, sl], bck, 4, 512, 512, sq, bsq, ssp, bss, rs, brs)
            P.op("dve", lambda e, sl=sl: e.tensor_tensor(out=ck[:, :, sl], in0=ck[:, :, sl],
                                                         in1=rs.unsqueeze(1).to_broadcast([128, 4, 512]), op=ALU.mult),
                 [bck, brs], [bck])
            for c in range(4):
                P.op("act", lambda e, sl=sl, c=c: e.activation(out=ck[:, c, sl], in_=ck[:, c, sl], func=AF.Copy,
                                                               scale=gk[:, c:c + 1]), [bck, bg], [bck])
            P.op("pool", lambda e, sl=sl: e.tensor_copy(out=ckb[:, :, sl], in_=ck[:, :, sl]), [bck], [bckb])
        for q in range(2):
            for c in range(4):
                g0 = 1024 + q * 1280
                tok_major_out(ck[:, c, g0:g0 + 256], bck, 128, O["o_ckv"][q, :, c * 128:(c + 1) * 128], 6)
        cc = A.alloc([128, 2, 512]); bcc = Buf()
        P.dma("sp", cc, I["cc_ckv"].rearrange("(j p) f -> p j f", p=128), [], [bcc])
        for c in range(4):
            for j in range(2):
                P.op("pe", lambda e, c=c, j=j: e.matmul(psum[:, 5, j * 128:(j + 1) * 128], lhsT=cc[:, j, c * 128:(c + 1) * 128], rhs=ident, start=True, stop=True),
                     [bcc, bconst], [pb[5]])
            P.op("act", lambda e, c=c: e.activation(out=ckb[:, c, T:TK], in_=psum[:, 5, 0:256], func=AF.Copy), [pb[5]], [bckb])
        ckr = A.alloc([128, 2, 64]); bckr = Buf(); krc = A.alloc([64, 256], BF16); bkrc = Buf()
        P.dma("sp", ckr, I["cc_kr"].rearrange("(j p) f -> p j f", p=128), [], [bckr])
        for j in range(2):
            P.op("pe", lambda e, j=j: e.matmul(psum[0:64, 5, j * 128:(j + 1) * 128], lhsT=ckr[:, j, :], rhs=ident, start=True, stop=True), [bckr, bconst], [pb[5]])
        P.op("act", lambda e: e.activation(out=krc, in_=psum[0:64, 5, 0:256], func=AF.Copy), [pb[5]], [bkrc])
        P.dma("pool", KRR[:, T:TK], krc, [bkrc], [])
        kst = [(A.alloc([128, TK], BF16), Buf()) for _ in range(2)]
        tt = [(o, 512) for o in range(0, 2560, 512)] + [(2560, 256)]
        cur = {}

        def epi_k(ci, c0, ncl, ti, t0, n, ps, bps):
            st_, bs_ = kst[ci % 2]
            if (ci + ti) % 2 == 0:
                P.op("act", lambda e: e.activation(out=st_[:, t0:t0 + n], in_=ps, func=AF.Copy), [bps], [bs_])
            else:
                P.op("dve", lambda e: e.tensor_copy(out=st_[:, t0:t0 + n], in_=ps), [bps], [bs_])
            if ti == len(tt) - 1:
                P.dma("pool", KNT[ci * 128:(ci + 1) * 128, :], st_, [bs_], [])
        linear(ckb, bckb, 4, [("w_uk", h * 128, 128) for h in range(16)], tt, epi_k, [0, 1, 2, 3])
        wvf = A.alloc([128, 4, 2048]); wv = A.alloc([128, 4, 2048], BF16); bwv = Buf()
        P.dma("sp", wvf, W["w_uv"].rearrange("(c p) n -> p c n", p=128), [wbuf["w_uv"]], [bwv])
        P.op("dve", lambda e: e.tensor_copy(out=wv, in_=wvf), [bwv], [bwv])
        vst = [(A.alloc([128, 2048], BF16), Buf()) for _ in range(2)]
        bkc = 0
        for blk in range(22):
            v_, bv_ = vst[blk % 2]
            for ct in range(4):
                bk = bkc % 4
                bkc += 1
                for kc in range(4):
                    P.op("pe", lambda e, blk=blk, ct=ct, kc=kc, bk=bk: e.matmul(
                        psum[:, bk, :], lhsT=ckb[:, kc, blk * 128:(blk + 1) * 128], rhs=wv[:, kc, ct * 512:(ct + 1) * 512],
                        start=(kc == 0), stop=(kc == 3)), [bckb, bwv], [pb[bk]])
                if ct % 2 == 0:
                    P.op("act", lambda e, v_=v_, ct=ct, bk=bk: e.activation(out=v_[:, ct * 512:(ct + 1) * 512], in_=psum[:, bk, :],
                                                                           func=AF.Copy), [pb[bk]], [bv_])
                else:
                    P.op("dve", lambda e, v_=v_, ct=ct, bk=bk: e.tensor_copy(out=v_[:, ct * 512:(ct + 1) * 512], in_=psum[:, bk, :]),
                         [pb[bk]], [bv_])
            P.dma("pool", VTOK[blk * 128:(blk + 1) * 128, :], v_, [bv_], [])
        sub_reset(A.base)
        cqb = A.alloc([128, 8, T], BF16); bcqb = Buf()
        gq2 = A.alloc([128, 8]); bg2 = Buf()
        P.dma("sp", gq2, I["g_qnT"], [], [bg2])
        mkq = A.off
        cqt = [(A.alloc([128, 8, 512]), Buf()) for _ in range(2)]
        sq8 = A.alloc([128, 8, 512]); bsq8 = Buf(); ssp8 = A.alloc([128, 512]); bss8 = Buf()
        rs8 = A.alloc([128, 512]); brs8 = Buf()
        for t5 in range(5):
            sl = slice(t5 * 512, (t5 + 1) * 512)
            cq_, bcq_ = cqt[t5 % 2]
            P.dma("sp", cq_, CQT[:, sl].rearrange("(c p) t -> p c t", p=128), [], [bcq_])
            rstd_from(cq_, bcq_, 8, 512, 1024, sq8, bsq8, ssp8, bss8, rs8, brs8)
            P.op("dve", lambda e, cq_=cq_: e.tensor_tensor(out=cq_, in0=cq_, in1=rs8.unsqueeze(1).to_broadcast([128, 8, 512]),
                                                           op=ALU.mult), [bcq_, brs8], [bcq_])
            for c in range(8):
                P.op("act", lambda e, cq_=cq_, c=c, sl=sl: e.activation(out=cqb[:, c, sl], in_=cq_[:, c, :], func=AF.Copy,
                                                                        scale=gq2[:, c:c + 1]), [bcq_, bg2], [bcqb])
        sub_reset(mkq)
        cos6 = A.alloc([64, 2048]); sin6 = A.alloc([64, 2048]); btab = Buf()
        P.dma("sp", cos6, I["cos64"], [], [btab])
        P.dma("sp", sin6, I["sin64"], [], [btab])
        qst = [(A.alloc([128, T], BF16), Buf()) for _ in range(2)]
        xsr = [(A.alloc([128, 512]), Buf()) for _ in range(2)]
        t1r = [(A.alloc([128, 512]), Buf()) for _ in range(2)]
        cnt = {"x": 0}
        chunks = []
        for h in range(16):
            chunks.append(("w_uq", h * 192, 128))
            chunks.append(("w_uq", h * 192 + 128, 64))
        gtt = [(g0, n) for (g0, n, s0) in GT]

        def epi_q(ci, c0, ncl, ti, t0, n, ps, bps):
            h, isr = ci // 2, ci % 2
            st_, bs_ = qst[ci % 2]
            s0 = GT[ti][2]
            if isr and s0 >= 0:
                xs, bxs = xsr[cnt["x"] % 2]; t1, bt1 = t1r[cnt["x"] % 2]
                bk2 = 4 + cnt["x"] % 2
                cnt["x"] += 1
                rope_apply(ps, bps, 64, n, st_[0:64, t0:t0 + n], bs_, cos6[:, s0:s0 + n], sin6[:, s0:s0 + n], btab, rot64T,
                           xs, bxs, t1, bt1, bk2)
            else:
                if (ci + ti) % 2 == 0:
                    P.op("act", lambda e: e.activation(out=st_[0:ncl, t0:t0 + n], in_=ps, func=AF.Copy), [bps], [bs_])
                else:
                    P.op("dve", lambda e: e.tensor_copy(out=st_[0:ncl, t0:t0 + n], in_=ps), [bps], [bs_])
            if ti == len(gtt) - 1:
                if isr:
                    P.dma("pool", QRT[h * 64:(h + 1) * 64, :], st_[0:64, :], [bs_], [])
                else:
                    P.dma("pool", QNT[h * 128:(h + 1) * 128, :], st_, [bs_], [])
        linear(cqb, bcqb, 8, chunks, gtt, epi_q, [0, 1, 2, 3])
        phase_end()

    def attn_jobs():
        jobs = []
        lat = [b if b < 8 else b + 2 for b in range(16)]
        for g0 in (0, 512, 1280, 1792):
            jobs.append((g0, 512, lat + [20, 21]))
        for q in range(2):
            jobs.append((1024 + q * 1280, 256, [8 + 10 * q, 9 + 10 * q]))
        return jobs

    def phase_mla_attn():
        TK = T + 256
        scale = 192 ** -0.5
        krr = A.alloc([64, TK], BF16); bkrr = Buf()
        P.dma("sp", krr, KRR, [], [bkrr])
        kn = [(A.alloc([128, TK], BF16), Buf()) for _ in range(2)]
        vh = [(A.alloc([128, 22, 128], BF16), Buf()) for _ in range(2)]
        qn = [(A.alloc([128, T], BF16), Buf()) for _ in range(2)]
        qr = [(A.alloc([64, T], BF16), Buf()) for _ in range(2)]
        om = [(A.alloc([128, T], BF16), Buf()) for _ in range(2)]
        Er = [(A.alloc([128, 512], BF16), Buf()) for _ in range(3)]
        rc = [(A.alloc([128, 512]), Buf()) for _ in range(2)]
        ecnt = 0
        jb = 0
        jobs = attn_jobs()
        for h in range(16):
            s = h % 2
            kn_, bkn_ = kn[s]; vh_, bvh_ = vh[s]; qn_, bqn_ = qn[s]; qr_, bqr_ = qr[s]; om_, bom_ = om[s]
            P.dma("sp", kn_, KNT[h * 128:(h + 1) * 128, :], [], [bkn_])
            P.dma("sp", vh_, VTOK[:, h * 128:(h + 1) * 128].rearrange("(b p) d -> p b d", p=128), [], [bvh_])
            P.dma("sp", qn_, QNT[h * 128:(h + 1) * 128, :], [], [bqn_])
            P.dma("sp", qr_, QRT[h * 64:(h + 1) * 64, :], [], [bqr_])
            for (g0, n, keys) in jobs:
                ob = 3 + jb % 2
                db = 5 + jb % 2
                jb += 1
                for ki, kb in enumerate(keys):
                    sb_ = ecnt % 3
                    E_, bE_ = Er[sb_]
                    ecnt += 1
                    P.op("pe", lambda e, kn_=kn_, qn_=qn_, kb=kb, g0=g0, n=n, sb_=sb_: e.matmul(
                        psum[:, sb_, 0:n], lhsT=kn_[:, kb * 128:(kb + 1) * 128], rhs=qn_[:, g0:g0 + n], start=True, stop=False),
                        [bkn_, bqn_], [pb[sb_]])
                    P.op("pe", lambda e, qr_=qr_, kb=kb, g0=g0, n=n, sb_=sb_: e.matmul(
                        psum[:, sb_, 0:n], lhsT=krr[:, kb * 128:(kb + 1) * 128], rhs=qr_[:, g0:g0 + n], start=False, stop=True),
                        [bkrr, bqr_], [pb[sb_]])
                    P.op("act", lambda e, E_=E_, sb_=sb_, n=n: e.activation(out=E_[:, 0:n], in_=psum[:, sb_, 0:n], func=AF.Exp,
                                                                           scale=scale), [pb[sb_]], [bE_])
                    st, sp_ = (ki == 0), (ki == len(keys) - 1)
                    P.op("pe", lambda e, vh_=vh_, kb=kb, E_=E_, ob=ob, n=n, st=st, sp_=sp_: e.matmul(
                        psum[:, ob, 0:n], lhsT=vh_[:, kb, :], rhs=E_[:, 0:n], start=st, stop=sp_), [bvh_, bE_], [pb[ob]])
                    P.op("pe", lambda e, E_=E_, db=db, n=n, st=st, sp_=sp_: e.matmul(
                        psum[:, db, 0:n], lhsT=ones_b, rhs=E_[:, 0:n], start=st, stop=sp_), [bE_, bconst], [pb[db]])
                r_, br_ = rc[jb % 2]
                P.op("dve", lambda e, r_=r_, db=db, n=n: e.reciprocal(out=r_[:, 0:n], in_=psum[:, db, 0:n]), [pb[db]], [br_])
                P.op("dve", lambda e, r_=r_, ob=ob, om_=om_, g0=g0, n=n: e.tensor_tensor(
                    out=om_[:, g0:g0 + n], in0=psum[:, ob, 0:n], in1=r_[:, 0:n], op=ALU.mult), [pb[ob], br_], [bom_])
            P.dma("pool", MIXT[h * 128:(h + 1) * 128, :], om_, [bom_], [])
        phase_end()

    def phase_diff_attn():
        scale = 128 ** -0.5
        cdk = A.alloc([128, 2, 2048]); cdv = A.alloc([128, 2, 2048]); bcdk = Buf(); bcdv = Buf()
        P.dma("sp", cdk, I["cd_k"].rearrange("(j p) f -> p j f", p=128), [], [bcdk])
        P.dma("sp", cdv, I["cd_v"].rearrange("(j p) f -> p j f", p=128), [], [bcdv])
        kctx = A.alloc([128, 16, 256], BF16); bkc = Buf()
        vctx = A.alloc([128, 2, 2048], BF16); bvc = Buf()
        P.op("pool", lambda e: e.tensor_copy(out=vctx, in_=cdv), [bcdv], [bvc])
        for hm in range(16):
            for j in range(2):
                P.op("pe", lambda e, hm=hm, j=j: e.matmul(psum[:, 7, j * 128:(j + 1) * 128], lhsT=cdk[:, j, hm * 128:(hm + 1) * 128], rhs=ident, start=True, stop=True), [bcdk, bconst], [pb[7]])
            P.op("act", lambda e, hm=hm: e.activation(out=kctx[:, hm, :], in_=psum[:, 7, 0:256], func=AF.Copy), [pb[7]], [bkc])
        gsl = A.alloc([128, 2]); bgsl = Buf()
        P.dma("sp", gsl, I["g_sublnT"], [], [bgsl])
        P.op("dve", lambda e: e.tensor_scalar(out=gsl, in0=gsl, scalar1=1.0 - LAM_INIT, scalar2=None, op0=ALU.mult), [bgsl], [bgsl])
        vT = A.alloc([128, 2, T]); bvT = Buf()
        vtok = A.alloc([128, 20, 256], BF16); bvt = Buf()
        dq = [(A.alloc([128, T], BF16), Buf()) for _ in range(2)]
        dk = [(A.alloc([128, T], BF16), Buf()) for _ in range(2)]
        om = [(A.alloc([128, 2, T], BF16), Buf()) for _ in range(2)]
        Er = [(A.alloc([128, 512], BF16), Buf()) for _ in range(3)]
        rc = A.alloc([128, 512]); brc = Buf()
        R0 = A.alloc([128, 2, 512]); bR0 = Buf()
        R1 = A.alloc([128, 2, 512]); bR1 = Buf()
        sq = A.alloc([128, 2, 512]); bsq = Buf(); ssp = A.alloc([128, 512]); bss = Buf()
        rs = A.alloc([128, 512]); brs = Buf()
        ecnt = 0
        jobs = attn_jobs()
        for h in range(8):
            om_, bom_ = om[h % 2]
            P.dma("sp", vT, DVT[h * 256:(h + 1) * 256, :].rearrange("(c p) t -> p c t", p=128), [], [bvT])
            for blk in range(20):
                for e_ in range(2):
                    P.op("pe", lambda e, blk=blk, e_=e_: e.matmul(psum[:, 6, e_ * 128:(e_ + 1) * 128], lhsT=vT[:, e_, blk * 128:(blk + 1) * 128], rhs=ident, start=True, stop=True),
                         [bvT, bconst], [pb[6]])
                P.op("act", lambda e, blk=blk: e.activation(out=vtok[:, blk, :], in_=psum[:, 6, 0:256], func=AF.Copy), [pb[6]], [bvt])
            for m in range(2):
                dq_, bdq_ = dq[m]; dk_, bdk_ = dk[m]
                P.dma("sp", dq_, DQT[(h * 2 + m) * 128:(h * 2 + m + 1) * 128, :], [], [bdq_])
                P.dma("sp", dk_, DKT[(h * 2 + m) * 128:(h * 2 + m + 1) * 128, :], [], [bdk_])
            for (g0, n, keys) in jobs:
                for m in range(2):
                    dq_, bdq_ = dq[m]; dk_, bdk_ = dk[m]
                    for ki, kb in enumerate(keys):
                        sb_ = ecnt % 2
                        E_, bE_ = Er[ecnt % 3]
                        ecnt += 1
                        if kb < 20:
                            lhs = dk_[:, kb * 128:(kb + 1) * 128]; rd = [bdk_, bdq_]
                            v0 = vtok[:, kb, 0:128]; v1 = vtok[:, kb, 128:256]; rdv = [bvt]
                        else:
                            j = kb - 20
                            lhs = kctx[:, h * 2 + m, j * 128:(j + 1) * 128]; rd = [bkc, bdq_]
                            v0 = vctx[:, j, h * 256:h * 256 + 128]; v1 = vctx[:, j, h * 256 + 128:h * 256 + 256]; rdv = [bvc]
                        P.op("pe", lambda e, lhs=lhs, dq_=dq_, g0=g0, n=n, sb_=sb_: e.matmul(
                            psum[:, sb_, 0:n], lhsT=lhs, rhs=dq_[:, g0:g0 + n], start=True, stop=True), rd, [pb[sb_]])
                        P.op("act", lambda e, E_=E_, sb_=sb_, n=n: e.activation(out=E_[:, 0:n], in_=psum[:, sb_, 0:n], func=AF.Exp,
                                                                               scale=scale), [pb[sb_]], [bE_])
                        st, sp_ = (ki == 0), (ki == len(keys) - 1)
                        P.op("pe", lambda e, v0=v0, E_=E_, n=n, st=st, sp_=sp_: e.matmul(
                            psum[:, 2, 0:n], lhsT=v0, rhs=E_[:, 0:n], start=st, stop=sp_), rdv + [bE_], [pb[2]])
                        P.op("pe", lambda e, v1=v1, E_=E_, n=n, st=st, sp_=sp_: e.matmul(
                            psum[:, 3, 0:n], lhsT=v1, rhs=E_[:, 0:n], start=st, stop=sp_), rdv + [bE_], [pb[3]])
                        P.op("pe", lambda e, E_=E_, n=n, st=st, sp_=sp_: e.matmul(
                            psum[:, 4, 0:n], lhsT=ones_b, rhs=E_[:, 0:n], start=st, stop=sp_), [bE_, bconst], [pb[4]])
                    Rm, bRm = (R0, bR0) if m == 0 else (R1, bR1)
                    P.op("dve", lambda e, n=n: e.reciprocal(out=rc[:, 0:n], in_=psum[:, 4, 0:n]), [pb[4]], [brc])
                    P.op("dve", lambda e, Rm=Rm, n=n: e.tensor_tensor(out=Rm[:, 0, 0:n], in0=psum[:, 2, 0:n], in1=rc[:, 0:n],
                                                                      op=ALU.mult), [pb[2], brc], [bRm])
                    P.op("dve", lambda e, Rm=Rm, n=n: e.tensor_tensor(out=Rm[:, 1, 0:n], in0=psum[:, 3, 0:n], in1=rc[:, 0:n],
                                                                      op=ALU.mult), [pb[3], brc], [bRm])
                P.op("dve", lambda e, n=n: e.scalar_tensor_tensor(out=R0[:, :, 0:n], in0=R1[:, :, 0:n], scalar=neglam[:, 0:1],
                                                                  in1=R0[:, :, 0:n], op0=ALU.mult, op1=ALU.add),
                     [bR0, bR1, bconst], [bR0])
                rstd_from(R0[:, :, 0:n], bR0, 2, n, 256, sq[:, :, 0:n], bsq, ssp[:, 0:n], bss, rs[:, 0:n], brs, bank=7)
                P.op("dve", lambda e, n=n: e.tensor_tensor(out=R0[:, :, 0:n], in0=R0[:, :, 0:n],
                                                           in1=rs[:, 0:n].unsqueeze(1).to_broadcast([128, 2, n]), op=ALU.mult),
                     [bR0, brs], [bR0])
                for e_ in range(2):
                    P.op("act", lambda e, e_=e_, om_=om_, g0=g0, n=n: e.activation(
                        out=om_[:, e_, g0:g0 + n], in_=R0[:, e_, 0:n], func=AF.Copy, scale=gsl[:, e_:e_ + 1]), [bR0, bgsl], [bom_])
            P.dma("pool", MIXT[2048 + h * 256:2048 + (h + 1) * 256, :].rearrange("(e p) t -> p e t", p=128), om_, [bom_], [])
        phase_end()

    def run_layers(nlayers):
        try:
            run_layers_(nlayers)
        except StopIteration:
            pass

    def run_layers_(nlayers):
        phase_gather()
        phase_load_x()
        for layer in range(nlayers):
            phase_mods(layer)
            phase_inproj(layer, 0)
            phase_inproj(layer, 1)
            if layer == 0:
                phase_attn_even()
                phase_pool()
                wo = "w_out_even"
            else:
                phase_mla_prep()
                phase_mla_attn()
                phase_diff_attn()
                wo = "w_out_odd"
            phase_back(MIXT, KC, wo, 0, 1280, 1)
            phase_back(MIXT, KC, wo, 1280, 1280, 1)
            phase_ffn_up(layer, 0)
            phase_ffn_up(layer, 1)
            for t5 in range(5):
                phase_back(UT, 86, f"w_down{layer}", t5 * 512, 512, 2)
        phase_final()

    return nc, P, es, run_layers, locals()


def host_consts():
    c128, s128, R128 = rope_tables(128)
    c64, s64, R64 = rope_tables(64)
    k = np.arange(128)[:, None]
    i = np.arange(128)[None, :]
    L = 2592
    invcnt = np.ones((4, L), np.float32)
    for gi, half in enumerate((1, 2, 4, 8)):
        for (pos, S) in ((8, 2048), (2064, 256), (2328, 256)):
            t = np.arange(S)
            cnt = np.minimum(t + half, S) - np.maximum(t - half, 0)
            invcnt[gi, pos:pos + S] = 1.0 / cnt
    return {
        "cos128": c128, "sin128": s128, "cos64": c64, "sin64": s64,
        "rot128T": np.ascontiguousarray(R128.T), "rot64T": np.ascontiguousarray(R64.T),
        "ident": np.eye(128, dtype=np.float32),
        "mask_ge": (k >= i).astype(np.float32), "mask_le": (k <= i).astype(np.float32),
        "invcnt": invcnt,
    }


NLAYERS = 2
RUN_CORES = NCORES
DEBUG_NOW = False
DBG_CUT = 99
STOP_PHASE = None
_CACHE = {}


def kernel(x_prompt, x_sample, cache_a_k, cache_a_v, cache_c_ckv, cache_c_krope, cache_d_k, cache_d_v,
           c, c_ctx, w_mod, b_mod, g_pre_mix, g_post_mix, g_pre_ffn, g_post_ffn,
           w_in_even, w_out_even, a_sink, w_pool, pool_scale,
           w_in_odd, w_out_odd, g_q_norm, w_uq, g_kv_norm, w_uk, w_uv,
           lambda_q1, lambda_k1, lambda_q2, lambda_k2, g_subln,
           w_up, conv_w, conv_b, w_down):
    f = lambda a: np.ascontiguousarray(np.asarray(a, dtype=np.float32))
    nc, P, es, run_layers, _ = build_program()
    run_layers(NLAYERS)
    P.emit()
    consts = host_consts()
    big = {
        "w_in_even": w_in_even[0], "w_pool": np.asarray(w_pool[0]).reshape(1024, 256),
        "w_out_even": w_out_even[0], "w_down0": w_down[0],
        "w_in_odd": w_in_odd[0], "w_uq": w_uq[0], "w_uk": np.asarray(w_uk[0]).reshape(512, 2048),
        "w_uv": np.asarray(w_uv[0]).reshape(512, 2048), "w_out_odd": w_out_odd[0], "w_down1": w_down[1],
    }
    for l in range(2):
        for j in range(6):
            big[f"w_mod{l}_{j}"] = w_mod[l][:, j * 4096:(j + 1) * 4096]
        big[f"w_up{l}_g"] = w_up[l][:, :FF]
        big[f"w_up{l}_v"] = w_up[l][:, FF:]
    fullw = {"w_mod": f(w_mod), "w_in_even": f(w_in_even), "w_out_even": f(w_out_even), "w_pool": f(w_pool),
             "w_in_odd": f(w_in_odd), "w_out_odd": f(w_out_odd), "w_uq": f(w_uq), "w_uk": f(w_uk), "w_uv": f(w_uv),
             "w_up": f(w_up), "w_down": f(w_down)}
    g4 = np.stack([np.asarray(g_pre_mix), np.asarray(g_post_mix), np.asarray(g_pre_ffn), np.asarray(g_post_ffn)], 0)
    shared = {
        "b_modT": f(np.asarray(b_mod).reshape(2, 192, 128).transpose(0, 2, 1)),
        "g4T": f(g4.reshape(4, 2, KC, 128).transpose(1, 0, 3, 2)),
        "conv_wT": f(np.asarray(conv_w).reshape(2, 3, 172, 128).transpose(0, 1, 3, 2)),
        "conv_bT": f(np.asarray(conv_b).reshape(2, 172, 128).transpose(0, 2, 1)),
        "a_sink": f(np.asarray(a_sink).reshape(1, 24)),
        "pool_scaleT": f(np.asarray(pool_scale).reshape(8, 128).T),
        "g_qnT": f(np.asarray(g_q_norm).reshape(8, 128).T),
        "g_kvnT": f(np.asarray(g_kv_norm).reshape(4, 128).T),
        "g_sublnT": f(np.asarray(g_subln).reshape(2, 128).T),
        "lams": f(np.stack([np.asarray(lambda_q1)[0], np.asarray(lambda_k1)[0], np.asarray(lambda_q2)[0],
                            np.asarray(lambda_k2)[0]], 0)),
    }
    shared.update(consts)
    in_maps = []
    for core in range(RUN_CORES):
        m = dict(shared)
        m["xs"] = f(x_sample[core])
        m["xp"] = f(np.asarray(x_prompt[2 * core:2 * core + 2]).reshape(512, D))
        m["ca_k"] = f(np.asarray(cache_a_k[core, 0]).reshape(256, 768))
        m["ca_v"] = f(np.asarray(cache_a_v[core, 0]).reshape(256, 768))
        m["cc_ckv"] = f(np.asarray(cache_c_ckv[core, 0]).reshape(256, 512))
        m["cc_kr"] = f(np.asarray(cache_c_krope[core, 0]).reshape(256, 64))
        m["cd_k"] = f(np.asarray(cache_d_k[core, 0]).reshape(256, 2048))
        m["cd_v"] = f(np.asarray(cache_d_v[core, 0]).reshape(256, 2048))
        selv = np.zeros((1, 8), np.float32)
        selv[0, core] = 1.0
        m["sel"] = selv
        cond2 = np.stack([np.asarray(c[core]), np.asarray(c_ctx)], 0)
        m["condT"] = f(cond2.reshape(2, KC, 128).transpose(2, 1, 0))
        if SHARD_WEIGHTS:
            for name, K, N in BIG_W:
                r = K // NCORES
                m[name] = f(np.asarray(big[name])[core * r:(core + 1) * r])
        elif not DEBUG_NOW:
            m.update(fullw)
        in_maps.append(m)
    res = run_bass_kernel_spmd(nc, in_maps, core_ids=list(range(RUN_CORES)))
    es.close()
    R = list(res.results)
    while len(R) < NCORES:
        R.append(R[0])
    y_prompt = np.concatenate([R[i]["y_p"].reshape(2, 256, D) for i in range(NCORES)], 0)
    y_sample = np.stack([R[i]["y_s"] for i in range(NCORES)], 0)
    cat = lambda k, shp: np.concatenate([R[i][k] for i in range(NCORES)], 0).reshape(shp).astype(np.float32)
    return (y_prompt.astype(np.float32), y_sample.astype(np.float32),
            cat("o_ak", (16, 1, 256, 6, 128)), cat("o_av", (16, 1, 256, 6, 128)),
            cat("o_ckv", (16, 1, 256, 512)), cat("o_kr", (16, 1, 256, 64)),
            cat("o_dk", (16, 1, 256, 8, 2, 128)), cat("o_dv", (16, 1, 256, 8, 256)))
```
